# Optimizing a Trainium2 kernel written in Bass

```python
import jax, jax.numpy as jnp
from jax import lax
import numpy as np

D_MODEL = 1024
BATCH = 8
SEQ = 4096
DEPTH = 2
DEC_BATCH = 8
DEC_SEQ = 32
PAST_LEN = 1024

CHUNK = 64
D_MIX = D_MODEL
A_WIDTH = D_MIX // 2
A_HEADS = 8
A_HD = A_WIDTH // A_HEADS
A_CONV = 4
RG_C = 8.0
B_WIDTH = D_MIX // 4
B_CONV = 3
C_WIDTH = D_MIX // 4
C_HEADS = 4
C_HD = C_WIDTH // C_HEADS
MLP_CHUNK = 128
N_MEM = 256
X_HEADS = 4
X_HD = D_MODEL // X_HEADS
D_FF = ((8 * D_MODEL // 3 + 127) // 128) * 128
FFN_CONV = 3
EPS = 1e-6
IN_COLS = 2 * A_WIDTH + 3 * B_WIDTH + 2 * C_WIDTH
SPLITS = (A_WIDTH, 2 * A_WIDTH, 2 * A_WIDTH + B_WIDTH, 2 * A_WIDTH + 2 * B_WIDTH,
          2 * A_WIDTH + 3 * B_WIDTH, 2 * A_WIDTH + 3 * B_WIDTH + C_WIDTH)

kernel_name = "hybrid_stream_rglru_shortconv_chunkmlp_step"


def rms_norm(x, g):
    x32 = x.astype(jnp.float32)
    y = x32 * lax.rsqrt(jnp.mean(x32 * x32, axis=-1, keepdims=True) + EPS)
    return (y * g.astype(jnp.float32)).astype(x.dtype)


def group_rms_norm(x, g):
    bsz, t, _ = x.shape
    xg = x.reshape(bsz, t, C_HEADS, C_HD).astype(jnp.float32)
    y = xg * lax.rsqrt(jnp.mean(xg * xg, axis=-1, keepdims=True) + EPS)
    return (y.reshape(bsz, t, C_WIDTH) * g.astype(jnp.float32)).astype(x.dtype)


def causal_dwconv(x, prev, w):
    width = w.shape[0]
    t = x.shape[1]
    xp = jnp.concatenate([prev.astype(x.dtype), x], axis=1)
    y = xp[:, 0:t] * w[0]
    for k in range(1, width):
        y = y + xp[:, k:k + t] * w[k]
    return y, xp[:, t:]


def rg_lru(x, h0, w_r, b_r, w_i, b_i, lam):
    bsz, t, _ = x.shape
    f32 = jnp.float32
    x32 = x.astype(f32)
    xh = x32.reshape(bsz, t, A_HEADS, A_HD)
    r = jax.nn.sigmoid(jnp.einsum('bthi,hij->bthj', xh, w_r.astype(f32)).reshape(bsz, t, A_WIDTH) + b_r.astype(f32))
    gi = jax.nn.sigmoid(jnp.einsum('bthi,hij->bthj', xh, w_i.astype(f32)).reshape(bsz, t, A_WIDTH) + b_i.astype(f32))
    log_a = -RG_C * r * jax.nn.softplus(-lam.astype(f32))
    a = jnp.exp(log_a)
    b = jnp.sqrt(-jnp.expm1(2.0 * log_a)) * (gi * x32)
    b = b.at[:, 0].add(a[:, 0] * h0.astype(f32))

    def combine(left, right):
        a_l, b_l = left
        a_r, b_rr = right
        return a_l * a_r, a_r * b_l + b_rr

    _, h = lax.associative_scan(combine, (a, b), axis=1)
    return h.astype(x.dtype), h[:, -1].astype(x.dtype)


def chunk_spatial_gate(u, v, w_s, b_s):
    bsz, t, _ = v.shape
    L = min(t, MLP_CHUNK)
    n = t // L
    mask = jnp.tril(jnp.ones((L, L), dtype=bool))
    w = jnp.where(mask, w_s[:, :L, :L], 0)
    vc = v.reshape(bsz, n, L, C_HEADS, C_HD)
    bias = jnp.transpose(b_s[:, :L])[None, None, :, :, None]
    mixed = jnp.einsum('hts,bnshc->bnthc', w, vc) + bias
    return u * mixed.reshape(bsz, t, C_WIDTH)


def memory_kv(mem, w_k, w_v):
    bsz, m, _ = mem.shape
    k = (mem @ w_k).reshape(bsz, m, X_HEADS, X_HD)
    v = (mem @ w_v).reshape(bsz, m, X_HEADS, X_HD)
    return k, v


def cross_attend(xn, k, v, w_q, w_o):
    bsz, t, _ = xn.shape
    f32 = jnp.float32
    q = (xn @ w_q).reshape(bsz, t, X_HEADS, X_HD)
    s = jnp.einsum('bthd,bmhd->bhtm', q.astype(f32), k.astype(f32)) * (X_HD ** -0.5)
    p = jax.nn.softmax(s, axis=-1)
    o = jnp.einsum('bhtm,bmhd->bthd', p, v.astype(f32)).astype(xn.dtype)
    return o.reshape(bsz, t, D_MODEL) @ w_o


def trunk_layer(x, mem_k, mem_v, conv_a_prev, h_prev, conv_b_prev, ffn_prev, lp):
    xn = rms_norm(x, lp['g_mix'])
    z = xn @ lp['w_in']
    xa, ga, xb, gb, gc, uc, vc = jnp.split(z, SPLITS, axis=-1)
    xa, conv_a_new = causal_dwconv(xa, conv_a_prev, lp['conv_a_w'])
    xa = xa + lp['conv_a_b']
    ha, h_new = rg_lru(xa, h_prev, lp['w_rg'], lp['b_rg'], lp['w_ig'], lp['b_ig'], lp['lam'])
    y_a = jax.nn.gelu(ga) * ha
    zb, conv_b_new = causal_dwconv(gc * xb, conv_b_prev, lp['conv_b_w'])
    y_b = gb * zb
    uc = jax.nn.gelu(uc)
    vc = group_rms_norm(jax.nn.gelu(vc), lp['g_v'])
    y_c = chunk_spatial_gate(uc, vc, lp['w_s'], lp['b_s'])
    x = x + jnp.concatenate([y_a, y_b, y_c], axis=-1) @ lp['w_out']
    x = x + cross_attend(rms_norm(x, lp['g_x']), mem_k, mem_v, lp['w_q'], lp['w_o'])
    gu = rms_norm(x, lp['g_ffn']) @ lp['w_up']
    g, u = jnp.split(gu, 2, axis=-1)
    g, ffn_new = causal_dwconv(g, ffn_prev, lp['conv_f_w'])
    x = x + (jax.nn.silu(g) * u) @ lp['w_down']
    return x, conv_a_new, h_new, conv_b_new, ffn_new, vc


def setup_inputs(seed: int = 0) -> dict:
    key = jax.random.key(seed)
    ks = iter(jax.random.split(key, 40))

    def nrm(shape, scale):
        return jax.random.normal(next(ks), shape, jnp.float32) * scale

    def gain(shape):
        return 1.0 + nrm(shape, 0.05)

    u = jax.random.uniform(next(ks), (DEPTH, A_WIDTH), jnp.float32, minval=0.9, maxval=0.999)
    a_base = u ** (1.0 / RG_C)
    lam = jnp.log(a_base) - jnp.log1p(-a_base)
    return {
        "x_prompt": nrm((BATCH, SEQ, D_MODEL), 1.0),
        "x_sample": nrm((DEC_BATCH, DEC_SEQ, D_MODEL), 1.0),
        "mem_prompt": nrm((BATCH, N_MEM, D_MODEL), 1.0),
        "cache_mem_k": nrm((DEPTH, DEC_BATCH, N_MEM, X_HEADS, X_HD), 1.0),
        "cache_mem_v": nrm((DEPTH, DEC_BATCH, N_MEM, X_HEADS, X_HD), 1.0),
        "state_conv_a": nrm((DEPTH, DEC_BATCH, A_CONV - 1, A_WIDTH), 1.0),
        "state_h_a": nrm((DEPTH, DEC_BATCH, A_WIDTH), 0.5),
        "state_conv_b": nrm((DEPTH, DEC_BATCH, B_CONV - 1, B_WIDTH), 1.0),
        "state_conv_ffn": nrm((DEPTH, DEC_BATCH, FFN_CONV - 1, D_FF), 1.0),
        "g_mix": gain((DEPTH, D_MODEL)),
        "w_in": nrm((DEPTH, D_MODEL, IN_COLS), D_MODEL ** -0.5),
        "conv_a_w": nrm((DEPTH, A_CONV, A_WIDTH), A_CONV ** -0.5),
        "conv_a_b": nrm((DEPTH, A_WIDTH), 0.02),
        "w_rg": nrm((DEPTH, A_HEADS, A_HD, A_HD), A_HD ** -0.5),
        "b_rg": nrm((DEPTH, A_WIDTH), 0.02),
        "w_ig": nrm((DEPTH, A_HEADS, A_HD, A_HD), A_HD ** -0.5),
        "b_ig": nrm((DEPTH, A_WIDTH), 0.02),
        "lam": lam,
        "conv_b_w": nrm((DEPTH, B_CONV, B_WIDTH), B_CONV ** -0.5),
        "g_v": gain((DEPTH, C_WIDTH)),
        "w_s": nrm((DEPTH, C_HEADS, MLP_CHUNK, MLP_CHUNK), 0.5 * MLP_CHUNK ** -0.5),
        "b_s": 1.0 + nrm((DEPTH, C_HEADS, MLP_CHUNK), 0.1),
        "w_out": nrm((DEPTH, D_MIX, D_MODEL), D_MIX ** -0.5),
        "g_x": gain((DEPTH, D_MODEL)),
        "w_q": nrm((DEPTH, D_MODEL, D_MODEL), D_MODEL ** -0.5),
        "w_k": nrm((DEPTH, D_MODEL, D_MODEL), D_MODEL ** -0.5),
        "w_v": nrm((DEPTH, D_MODEL, D_MODEL), D_MODEL ** -0.5),
        "w_o": nrm((DEPTH, D_MODEL, D_MODEL), D_MODEL ** -0.5),
        "g_ffn": gain((DEPTH, D_MODEL)),
        "w_up": nrm((DEPTH, D_MODEL, 2 * D_FF), D_MODEL ** -0.5),
        "conv_f_w": nrm((DEPTH, FFN_CONV, D_FF), FFN_CONV ** -0.5),
        "w_down": nrm((DEPTH, D_FF, D_MODEL), D_FF ** -0.5),
        "g_final": gain((D_MODEL,)),
    }


def reference(x_prompt, x_sample, mem_prompt, cache_mem_k, cache_mem_v, state_conv_a, state_h_a,
              state_conv_b, state_conv_ffn, g_mix, w_in, conv_a_w, conv_a_b, w_rg, b_rg, w_ig, b_ig,
              lam, conv_b_w, g_v, w_s, b_s, w_out, g_x, w_q, w_k, w_v, w_o, g_ffn, w_up, conv_f_w,
              w_down, g_final):
    bp = x_prompt.shape[0]
    xp, xs = x_prompt, x_sample
    p_ca, p_h, p_cb, p_cf, p_mk, p_mv = [], [], [], [], [], []
    s_ca, s_h, s_cb, s_cf, s_vc = [], [], [], [], []
    for l in range(DEPTH):
        lp = dict(g_mix=g_mix[l], w_in=w_in[l], conv_a_w=conv_a_w[l], conv_a_b=conv_a_b[l],
                  w_rg=w_rg[l], b_rg=b_rg[l], w_ig=w_ig[l], b_ig=b_ig[l], lam=lam[l],
                  conv_b_w=conv_b_w[l], g_v=g_v[l], w_s=w_s[l], b_s=b_s[l], w_out=w_out[l],
                  g_x=g_x[l], w_q=w_q[l], w_o=w_o[l], g_ffn=g_ffn[l], w_up=w_up[l],
                  conv_f_w=conv_f_w[l], w_down=w_down[l])
        mk, mv = memory_kv(mem_prompt, w_k[l], w_v[l])
        dt = xp.dtype
        xp, ca, hh, cb, cf, _ = trunk_layer(
            xp, mk, mv,
            jnp.zeros((bp, A_CONV - 1, A_WIDTH), dt), jnp.zeros((bp, A_WIDTH), dt),
            jnp.zeros((bp, B_CONV - 1, B_WIDTH), dt), jnp.zeros((bp, FFN_CONV - 1, D_FF), dt), lp)
        p_ca.append(ca); p_h.append(hh); p_cb.append(cb); p_cf.append(cf); p_mk.append(mk); p_mv.append(mv)
        xs, ca2, hh2, cb2, cf2, vc2 = trunk_layer(
            xs, cache_mem_k[l], cache_mem_v[l], state_conv_a[l], state_h_a[l],
            state_conv_b[l], state_conv_ffn[l], lp)
        s_ca.append(ca2); s_h.append(hh2); s_cb.append(cb2); s_cf.append(cf2); s_vc.append(vc2)
    y_prompt = rms_norm(xp, g_final)
    y_sample = rms_norm(xs, g_final)
    return (y_prompt, y_sample,
            jnp.stack(p_ca), jnp.stack(p_h), jnp.stack(p_cb), jnp.stack(p_cf), jnp.stack(p_mk), jnp.stack(p_mv),
            jnp.stack(s_ca), jnp.stack(s_h), jnp.stack(s_cb), jnp.stack(s_cf), jnp.stack(s_vc))
```

```python
import numpy as np
from contextlib import ExitStack
import concourse.bass as bass
import concourse.mybir as mybir
from concourse.bass_utils import run_bass_kernel_spmd

F32 = mybir.dt.float32
BF16 = mybir.dt.bfloat16
ALU = mybir.AluOpType
AF = mybir.ActivationFunctionType
AX = mybir.AxisListType

D = 1024
SEQ = 4096
NT = 512
DEC = 32
NMEM = 256
DFF = 2816
NFF = 22
EPS = 1e-6
NSLOT = 4
SLOTW = 2048
GRAN = 512
_ISZ = {F32: 4, BF16: 2}


PSUM_BASE = 10_000_000
SMALL_END = [0]
GRAN_S = 16


def ap_keys(ap):
    space = str(ap.space).upper()
    sb = "SB" in space
    isz = _ISZ[ap.dtype]
    dims = ap.ap
    pstep = dims[0][0]
    off = ap.offset % pstep if pstep > 0 else ap.offset
    free = list(dims[1:]) or [(1, 1)]
    last_step, last_cnt = free[-1]
    run = (last_cnt - 1) * abs(last_step) + 1
    outer = free[:-1]
    keys = set()
    starts = [off]
    for step, cnt in outer:
        if step == 0 or cnt == 1:
            continue
        starts = [s + i * step for s in starts for i in range(cnt)]
    if sb and off * isz < SMALL_END[0]:
        base, gran = 20_000_000, GRAN_S
    elif sb:
        base, gran = 0, GRAN
    else:
        base, gran = PSUM_BASE, 2048
    for s in starts:
        lo = s * isz
        hi = (s + run) * isz
        for g in range(lo // gran, (hi - 1) // gran + 1):
            keys.add(base + g)
    return keys


class Op:
    __slots__ = ("fn", "waits", "inc", "sem", "val", "dma")


class Sched:
    ENGS = ("pe", "act", "dve", "pool", "sp")

    def __init__(self, nc, es):
        self.nc = nc
        n_dma = {"sp": 24, "pool": 24, "act": 8}
        self.sems = []

        def new_sem(name):
            self.sems.append(es.enter_context(nc.semaphore(name)))
            return len(self.sems) - 1

        self.eng_sem = {e: new_sem("s_" + e) for e in self.ENGS}
        self.dma_pool = {e: [new_sem(f"d_{e}{i}") for i in range(n)] for e, n in n_dma.items()}
        self.dma_rr = {e: 0 for e in self.dma_pool}
        self.dma_val = {}
        self.ops = {e: [] for e in self.ENGS}
        self.count = {e: 0 for e in self.ENGS}
        self.last_inc = {e: True for e in self.ENGS}
        self.last_w = {}
        self.readers = {}
        self.seen = {e: {} for e in self.ENGS}

    def add(self, eng, fn, reads=(), writes=(), rkeys=(), wkeys=(), inc=True, dma=False):
        kr = set(rkeys)
        for ap in reads:
            kr |= ap_keys(ap)
        kw = set(wkeys)
        for ap in writes:
            kw |= ap_keys(ap)
        op = Op()
        op.fn = fn
        op.dma = dma
        op.inc = inc
        if dma:
            pool = self.dma_pool[eng]
            own = pool[self.dma_rr[eng] % len(pool)]
            self.dma_rr[eng] += 1
        else:
            own = self.eng_sem[eng]
        deps = {}
        last_w = self.last_w
        readers = self.readers
        for k in kr:
            lw = last_w.get(k)
            if lw is not None:
                s, v = lw
                if (s != own or eng != "pe") and deps.get(s, 0) < v:
                    deps[s] = v
            if isinstance(k, int) and PSUM_BASE <= k < 2 * PSUM_BASE:
                rd = readers.get(k)
                if rd:
                    for s, v in rd.items():
                        if s != own and deps.get(s, 0) < v:
                            deps[s] = v
        for k in kw:
            lw = last_w.get(k)
            if lw is not None:
                s, v = lw
                if s != own and deps.get(s, 0) < v:
                    deps[s] = v
            rd = readers.get(k)
            if rd:
                for s, v in rd.items():
                    if s != own and deps.get(s, 0) < v:
                        deps[s] = v
        if dma:
            prev = self.dma_val.get(own, 0)
            if prev and deps.get(own, 0) < prev:
                deps[own] = prev
        seen = self.seen[eng]
        waits = []
        for s, v in deps.items():
            if seen.get(s, 0) < v:
                waits.append((s, v))
                seen[s] = v
        op.waits = waits
        if dma:
            val = self.dma_val.get(own, 0) + 16
            self.dma_val[own] = val
        elif inc:
            self.count[eng] += 1
            val = self.count[eng]
            self.last_inc[eng] = True
        else:
            val = self.count[eng] + 1
            self.last_inc[eng] = False
        op.sem = own
        op.val = val
        for k in kw:
            last_w[k] = (own, val)
            readers[k] = {}
        for k in kr:
            rd = readers.get(k)
            if rd is None:
                rd = readers[k] = {}
            if rd.get(own, 0) < val:
                rd[own] = val
        self.ops[eng].append(op)
        return op

    def emit(self, block):
        for e in self.ENGS:
            assert self.last_inc[e], f"engine {e} ends with non-inc op"
        sems = self.sems

        def run(ename, eng):
            own = sems[self.eng_sem[ename]]
            for op in self.ops[ename]:
                for s, v in op.waits:
                    eng.wait_ge(sems[s], v)
                ins = op.fn(eng)
                if op.dma:
                    ins.then_inc(sems[op.sem], 16)
                elif op.inc:
                    ins.then_inc(own, 1)
            for s in self.dma_pool.get(ename, []):
                v = self.dma_val.get(s, 0)
                if v:
                    eng.wait_ge(sems[s], v)

        block.tensor(lambda eng: run("pe", eng))
        block.scalar(lambda eng: run("act", eng))
        block.vector(lambda eng: run("dve", eng))
        block.gpsimd(lambda eng: run("pool", eng))
        block.sync(lambda eng: run("sp", eng))


W_NAMES = ["g_mix", "w_in", "conv_a_w", "conv_a_b", "w_rg", "b_rg", "w_ig", "b_ig", "lam", "conv_b_w", "g_v",
           "w_s", "b_s", "w_out", "g_x", "w_q", "w_k", "w_v", "w_o", "g_ffn", "w_up", "conv_f_w", "w_down", "g_final"]
W_SHAPES = {
    "g_mix": [2, D], "w_in": [2, D, 2304], "conv_a_w": [2, 4, 512], "conv_a_b": [2, 512], "w_rg": [2, 8, 64, 64],
    "b_rg": [2, 512], "w_ig": [2, 8, 64, 64], "b_ig": [2, 512], "lam": [2, 512], "conv_b_w": [2, 3, 256],
    "g_v": [2, 256], "w_s": [2, 4, 128, 128], "b_s": [2, 4, 128], "w_out": [2, D, D], "g_x": [2, D],
    "w_q": [2, D, D], "w_k": [2, D, D], "w_v": [2, D, D], "w_o": [2, D, D], "g_ffn": [2, D],
    "w_up": [2, D, 2 * DFF], "conv_f_w": [2, 3, DFF], "w_down": [2, DFF, D], "g_final": [D],
}
IN_SHAPES = {
    "xp": [SEQ, D], "xs": [DEC, D], "mem": [NMEM, D], "ck": [2, NMEM, D], "cv": [2, NMEM, D],
    "sca": [2, 3, 512], "sh": [2, 512], "scb": [2, 2, 256], "scf": [2, 2, DFF],
}
OUT_SHAPES = {
    "yp": [SEQ, D], "ys": [DEC, D], "p_ca": [2, 3, 512], "p_h": [2, 512], "p_cb": [2, 2, 256], "p_cf": [2, 2, DFF],
    "p_mk": [2, NMEM, D], "p_mv": [2, NMEM, D], "s_ca": [2, 3, 512], "s_h": [2, 512], "s_cb": [2, 2, 256],
    "s_cf": [2, 2, DFF], "s_vc": [2, DEC, 256],
}

WIN_COLS = [(0, 512), (512, 512), (1792, 512), (1024, 512), (1536, 256)]
PIECES = ([("win", i, 8, WIN_COLS[i][1]) for i in range(5)] + [("wout", i, 8, 512) for i in range(2)]
          + [("wq", i, 8, 512) for i in range(2)] + [("wo", i, 8, 512) for i in range(2)]
          + [("wup", i, 8, 512) for i in range(11)] + [("wdn", i, 22, 128) for i in range(8)])
NPIECE = len(PIECES)
PIECES_X = PIECES + [("wk", 0, 8, 512), ("wk", 1, 8, 512), ("wv", 0, 8, 512), ("wv", 1, 8, 512)]


def build_program(n_ptiles=SEQ // NT, do_sample=True):
    nc = bass.Bass("TRN2", target_bir_lowering=False)
    din = {k: nc.dram_tensor(k, s, F32, kind="ExternalInput").ap() for k, s in IN_SHAPES.items()}
    dw = {k: nc.dram_tensor(k, s, F32, kind="ExternalInput").ap() for k, s in W_SHAPES.items()}
    dout = {k: nc.dram_tensor(k, s, F32, kind="ExternalOutput").ap() for k, s in OUT_SHAPES.items()}
    wscr = nc.dram_tensor("wscr", [2, NPIECE, 128, 2 * SLOTW], BF16).ap()

    with ExitStack() as es:
        AW = 52600
        A = es.enter_context(nc.sbuf_tensor("arena", [128, AW], F32))
        PS = es.enter_context(nc.psum_tensor("ps", [128, 8, 512], F32))
        S = Sched(nc, es)
        pos = [0]
        maxpos = [0]

        def f32v(off, words):
            return A[:, off:off + words]

        def bfv(off, words):
            return A[:, off:off + words].bitcast(BF16)

        def alloc(words):
            pos[0] = (pos[0] + 127) // 128 * 128
            o = pos[0]
            pos[0] += words
            assert pos[0] <= AW, pos[0]
            maxpos[0] = max(maxpos[0], pos[0])
            return o

        def salloc(words):
            o = pos[0]
            pos[0] += (words + 3) // 4 * 4
            return o

        ident = f32v(salloc(128), 128)
        ones_div = bfv(salloc(64), 64)
        ones1 = bfv(salloc(64), 64)
        epsb = f32v(salloc(4), 2)
        onep = f32v(salloc(4), 2)
        WsT = [bfv(salloc(256), 256).rearrange("p (h t) -> p h t", h=4) for _ in range(2)]
        biasbc = [f32v(salloc(256), 256).rearrange("p (h t) -> p h t", h=2) for _ in range(2)]
        Wr = [bfv(salloc(256), 256).rearrange("p (c j) -> p c j", c=4) for _ in range(2)]
        Wi = [bfv(salloc(256), 256).rearrange("p (c j) -> p c j", c=4) for _ in range(2)]
        gvrep = [f32v(salloc(256), 256) for _ in range(2)]
        V = [f32v(salloc(128), 128) for _ in range(2)]
        Vf = f32v(salloc(128), 128)
        cvec = [f32v(salloc(8), 8) for _ in range(2)]
        ST = [f32v(salloc(128), 128) for _ in range(2)]
        ssb = f32v(salloc(16), 16).rearrange("p (b g) -> p b g", b=4)
        rsb = f32v(salloc(16), 16).rearrange("p (b g) -> p b g", b=4)
        pos[0] = (pos[0] + 127) // 128 * 128
        SMALL_END[0] = pos[0] * 4
        o_xfm = alloc(8 * NT)
        xfm = f32v(o_xfm, 8 * NT).rearrange("p (c n) -> p c n", c=8)
        o_xn = alloc(4 * NT)
        xn = bfv(o_xn, 4 * NT).rearrange("p (c n) -> p c n", c=8)
        o_yq = alloc(4 * NT)
        yq = bfv(o_yq, 4 * NT).rearrange("p (c n) -> p c n", c=8)
        rstd = f32v(alloc(NT), NT)
        lnb = f32v(alloc(NT), NT)
        xtok = [f32v(alloc(D), D) for _ in range(2)]
        ytok = [f32v(alloc(D), D) for _ in range(2)]
        kT = [bfv(alloc(1024), 1024).rearrange("p (c m) -> p c m", c=8) for _ in range(2)]
        vv = [bfv(alloc(1024), 1024).rearrange("p (c f) -> p c f", c=2) for _ in range(2)]
        slots = [alloc(SLOTW) for _ in range(NSLOT)]
        memT = bfv(alloc(1024), 1024).rearrange("p (c m) -> p c m", c=8)
        kvh = [f32v(alloc(NT), NT) for _ in range(2)]
        R0 = pos[0]
        o_xaraw = alloc(4 * 640)
        xa_raw = f32v(o_xaraw, 4 * 640).rearrange("p (c n) -> p c n", c=4)
        TA, T1, T2, T3, T4 = [f32v(alloc(4 * NT), 4 * NT).rearrange("p (c n) -> p c n", c=4) for _ in range(5)]
        xa_bf = bfv(alloc(2 * NT), 2 * NT).rearrange("p (c n) -> p c n", c=4)
        o_xb = alloc(2 * NT)
        xb_sb = f32v(o_xb, 2 * NT).rearrange("p (c n) -> p c n", c=2)
        zb_bf = bfv(o_xb, NT).rearrange("p (c n) -> p c n", c=2)
        t_raw = f32v(alloc(2 * 640), 2 * 640).rearrange("p (c n) -> p c n", c=2)
        zb = f32v(alloc(2 * NT), 2 * NT).rearrange("p (c n) -> p c n", c=2)
        vt = f32v(alloc(1024), 1024).rearrange("p (b f) -> p b f", b=4)
        vsq = f32v(alloc(1024), 1024).rearrange("p (b f) -> p b f", b=4)
        vn_bf = bfv(alloc(512), 512).rearrange("p (b f) -> p b f", b=4)
        vn32 = f32v(alloc(256), 256)
        o_tmpc = alloc(2 * NT)
        tmpc = bfv(o_tmpc, NT).rearrange("p (c n) -> p c n", c=2)
        R_end_mixer = pos[0]
        pos[0] = R0
        sq = bfv(alloc(4 * NT), 4 * NT).rearrange("p (c n) -> p c n", c=8)
        pT = bfv(alloc(4 * NT), 4 * NT).rearrange("p (h m n) -> p h m n", h=4, m=2)
        ob = bfv(alloc(4 * NT), 4 * NT).rearrange("p (c n) -> p c n", c=8)
        recip = [f32v(alloc(NT), NT) for _ in range(2)]
        pos[0] = R0 + 4 * NT
        hbuf = bfv(alloc(11 * NT), 11 * NT).rearrange("p (f n) -> p f n", f=NFF)
        g_raw = [f32v(alloc(640), 640) for _ in range(3)]
        accb = [f32v(alloc(NT), NT) for _ in range(3)]
        slb = [f32v(alloc(NT), NT) for _ in range(3)]
        pos[0] = R0 + 4 * NT
        yfin = f32v(alloc(8 * NT), 8 * NT).rearrange("p (c n) -> p c n", c=8)
        pos[0] = R0
        memtok = f32v(alloc(2048), 2048).rearrange("p (m f) -> p m f", m=2)
        stage = f32v(alloc(128), 128)
        stages = [f32v(alloc(128), 128) for _ in range(4)]
        wsls = [f32v(alloc(512), 512).rearrange("p (h s) -> p h s", h=4) for _ in range(2)]
        wst_fs = [f32v(alloc(512), 512).rearrange("p (h t) -> p h t", h=4) for _ in range(2)]
        bd_fs = [f32v(alloc(512), 512).rearrange("p (c j) -> p c j", c=4) for _ in range(4)]
        print('SBUF words used', maxpos[0], 'of', AW)

        bank_ctr = [0]
        bank_resv = set()

        def nb():
            while True:
                b = bank_ctr[0] % 8
                bank_ctr[0] += 1
                if b not in bank_resv:
                    return b

        def is_ap(x):
            return not isinstance(x, (int, float)) and x is not None

        def ACT(out, in_, func, bias=None, scale=None):
            rd = [in_] + [x for x in (bias, scale) if is_ap(x)]
            kw = {}
            if bias is not None:
                kw["bias"] = bias
            if scale is not None:
                kw["scale"] = scale
            S.add("act", lambda e: e.activation(out=out, in_=in_, func=func, **kw), reads=rd, writes=[out])

        def TT(eng, out, in0, in1, op):
            S.add(eng, lambda e: e.tensor_tensor(out=out, in0=in0, in1=in1, op=op), reads=[in0, in1], writes=[out])

        def TS(eng, out, in0, s1, s2, op0, op1=None):
            rd = [in0] + [x for x in (s1, s2) if is_ap(x)]
            if op1 is None:
                S.add(eng, lambda e: e.tensor_scalar(out=out, in0=in0, scalar1=s1, scalar2=None, op0=op0),
                      reads=rd, writes=[out])
            else:
                S.add(eng, lambda e: e.tensor_scalar(out=out, in0=in0, scalar1=s1, scalar2=s2, op0=op0, op1=op1),
                      reads=rd, writes=[out])

        def STT(out, in0, scalar, in1, op0, op1):
            rd = [in0, in1] + ([scalar] if is_ap(scalar) else [])
            S.add("dve", lambda e: e.scalar_tensor_tensor(out=out, in0=in0, scalar=scalar, in1=in1, op0=op0, op1=op1),
                  reads=rd, writes=[out])

        def CP(eng, out, in_):
            if eng == "act":
                S.add(eng, lambda e: e.activation(out=out, in_=in_, func=AF.Identity), reads=[in_], writes=[out])
            else:
                S.add(eng, lambda e: e.tensor_copy(out=out, in_=in_), reads=[in_], writes=[out])

        def MM(out, lhsT, rhs, start, stop, tp=None, wr=None):
            kw = {}
            if tp is not None:
                kw["tile_position"] = tp
            S.add("pe", lambda e: e.matmul(out, lhsT=lhsT, rhs=rhs, start=start, stop=stop, **kw),
                  reads=[lhsT, rhs], writes=[wr if wr is not None else out], inc=stop)

        def TR(out, in_, idn, inc=True):
            S.add("pe", lambda e: e.transpose(out=out, in_=in_, identity=idn), reads=[in_, idn], writes=[out], inc=inc)

        def DMA(q, out, in_, reads=(), writes=(), rkeys=(), wkeys=(), slow=False):
            if slow:
                S.add(q, lambda e: e.dma_start(out=out, in_=in_, allow_slow_non_contiguous=True),
                      reads=reads, writes=writes, rkeys=rkeys, wkeys=wkeys, dma=True)
            else:
                S.add(q, lambda e: e.dma_start(out=out, in_=in_), reads=reads, writes=writes, rkeys=rkeys,
                      wkeys=wkeys, dma=True)

        def MEMSET(eng, ap, val):
            S.add(eng, lambda e: e.memset(ap, val), writes=[ap])

        groups = [("p", i) for i in range(n_ptiles)] + ([("s", 0)] if do_sample else [])
        base_order = list(range(NPIECE))
        first_order = base_order[:7] + [NPIECE, NPIECE + 1, NPIECE + 2, NPIECE + 3] + base_order[7:]
        seq = [(gi, l, pi) for gi in range(len(groups)) for l in range(2)
               for pi in (first_order if gi == 0 else base_order)]
        wstate = {"loaded": 0, "cur": 0}

        def src_aps(l, pi):
            kind, i, kc, ncol = PIECES_X[pi]
            if kind in ("wk", "wv"):
                nm = {"wk": "w_k", "wv": "w_v"}[kind]
                return [(dw[nm][l][:, i * 512:(i + 1) * 512].rearrange("(k p) n -> p k n", p=128), None)]
            if kind == "win":
                c0 = WIN_COLS[i][0]
                return [(dw["w_in"][l][:, c0:c0 + ncol].rearrange("(k p) n -> p k n", p=128), None)]
            if kind in ("wout", "wq", "wo"):
                nm = {"wout": "w_out", "wq": "w_q", "wo": "w_o"}[kind]
                return [(dw[nm][l][:, i * 512:(i + 1) * 512].rearrange("(k p) n -> p k n", p=128), None)]
            if kind == "wup":
                return [(dw["w_up"][l][:, i * 256:(i + 1) * 256].rearrange("(k p) n -> p k n", p=128), 0),
                        (dw["w_up"][l][:, DFF + i * 256:DFF + (i + 1) * 256].rearrange("(k p) n -> p k n", p=128), 1)]
            if kind == "wdn":
                return [(dw["w_down"][l][:, i * 128:(i + 1) * 128].rearrange("(k p) n -> p k n", p=128), None)]
            raise ValueError(kind)

        def slot_view(sidx, pi):
            kind, i, kc, ncol = PIECES_X[pi]
            nel = kc * ncol
            flat = bfv(slots[sidx], (nel + 1) // 2)
            if kind == "wup":
                return flat, flat.rearrange("p (k h n) -> p k h n", k=8, h=2)
            return flat, flat.rearrange("p (k n) -> p k n", k=kc)

        def record_load(idx):
            gi, l, pi = seq[idx]
            sidx = idx % NSLOT
            flat, view = slot_view(sidx, pi)
            kind, i, kc, ncol = PIECES_X[pi]
            nel = kc * ncol
            wb_tile = min(pi % 3, n_ptiles - 1)
            if gi <= wb_tile and groups[gi][0] == "p":
                for src, half in src_aps(l, pi):
                    dst = view if half is None else view[:, :, half, :]
                    DMA("pool", dst, src, writes=[dst])
                if pi < NPIECE and gi == wb_tile:
                    wb = lambda: DMA("sp", wscr[l, pi, :, 0:nel], flat, reads=[flat], wkeys=[("scr", l, pi)])
                    if wstate.get("defer") is not None:
                        wstate["defer"].append(wb)
                    else:
                        wb()
            else:
                DMA("sp", flat, wscr[l, pi, :, 0:nel], writes=[flat], rkeys=[("scr", l, pi)])

        def use_piece(gi, l, pi, hold=0):
            idx = wstate["cur"]
            assert seq[idx] == (gi, l, pi), (seq[idx], gi, l, pi)
            while wstate["loaded"] < min(len(seq), idx + NSLOT - hold):
                record_load(wstate["loaded"])
                wstate["loaded"] += 1
            wstate["cur"] += 1
            return slot_view(idx % NSLOT, pi)[1]

        MEMSET("pool", ident, 0.0)
        S.add("pool", lambda e: e.affine_select(out=ident, in_=ident, compare_op=ALU.not_equal, fill=1.0, base=0,
                                                pattern=[[-1, 128]], channel_multiplier=1),
              reads=[ident], writes=[ident])
        MEMSET("pool", ones_div, 1.0 / 1024)
        MEMSET("pool", ones1, 1.0)
        MEMSET("pool", epsb, EPS)
        MEMSET("pool", onep, 1.0 + 2.0 ** -23)
        MEMSET("pool", ST[0], 0.0)

        VG_MIX, VG_X, VG_FFN, VCAW, VCAB, VBR, VBI, VLAM, VCBW, VCFW = 0, 8, 16, 24, 40, 44, 48, 52, 56, 62
        SCA, SH, SCB, SCF = 0, 12, 16, 20

        const_state = {}

        def setup_consts_dma():
            stage_jobs = []
            cq_ctr = [0]

            def cq():
                cq_ctr[0] += 1
                return "sp" if cq_ctr[0] % 2 else "act"

            def stage_rows(rows, dst):
                stg = stages[len(stage_jobs)]
                r0 = 0
                for ap in rows:
                    n = ap.shape[0]
                    DMA(cq(), stg[r0:r0 + n, :], ap, writes=[stg[r0:r0 + n, :]])
                    r0 += n
                stage_jobs.append((stg, r0, dst))

            for l in range(2):
                rows = [dw["g_mix"][l].rearrange("(c p) -> c p", p=128), dw["g_x"][l].rearrange("(c p) -> c p", p=128),
                        dw["g_ffn"][l].rearrange("(c p) -> c p", p=128),
                        dw["conv_a_w"][l].rearrange("k (c p) -> (k c) p", p=128),
                        dw["conv_a_b"][l].rearrange("(c p) -> c p", p=128), dw["b_rg"][l].rearrange("(c p) -> c p", p=128),
                        dw["b_ig"][l].rearrange("(c p) -> c p", p=128), dw["lam"][l].rearrange("(c p) -> c p", p=128),
                        dw["conv_b_w"][l].rearrange("k (c p) -> (k c) p", p=128),
                        dw["conv_f_w"][l].rearrange("k (c p) -> (k c) p", p=128)]
                stage_rows(rows, V[l])
            stage_rows([dw["g_final"].rearrange("(c p) -> c p", p=128)], Vf)
            if do_sample:
                rows = []
                for l in range(2):
                    rows += [din["sca"][l].rearrange("t (c p) -> (t c) p", p=128), din["sh"][l].rearrange("(c p) -> c p", p=128),
                             din["scb"][l].rearrange("t (c p) -> (t c) p", p=128),
                             din["scf"][l].rearrange("t (c p) -> (t c) p", p=128)]
                stage_rows(rows, ST[1])
            const_state["stage_jobs"] = stage_jobs
            for l in range(2):
                for wi, wname in enumerate(("w_rg", "w_ig")):
                    bd = bd_fs[2 * l + wi]
                    MEMSET("pool", bd, 0.0)
                    for hh in range(2):
                        src = dw[wname][l].rearrange("(c h) i j -> h i c j", h=2)[hh]
                        d = bd[hh * 64:(hh + 1) * 64, :, hh * 64:(hh + 1) * 64]
                        DMA(cq(), d, src, writes=[d])
                DMA(cq(), wsls[l], dw["w_s"][l].rearrange("h t s -> t h s"), writes=[wsls[l]])
                for hp in range(2):
                    for hh in range(2):
                        d = biasbc[l][hh * 64:(hh + 1) * 64, hp, :]
                        DMA(cq(), d, dw["b_s"][l][2 * hp + hh].partition_broadcast(64), writes=[d])
                DMA(cq(), gvrep[l], dw["g_v"][l].partition_broadcast(128), writes=[gvrep[l]])
            DMA("sp", memtok, din["mem"].rearrange("(m p) f -> p m f", p=128), writes=[memtok])

        def setup_consts_compute():
            for stg, r0, dst in const_state["stage_jobs"]:
                b = nb()
                TR(PS[:, b, 0:r0], stg[0:r0, :], ident[0:r0, 0:r0])
                CP("dve", dst[:, 0:r0], PS[:, b, 0:r0])
            for l in range(2):
                ACT(cvec[l][:, 0:4], V[l][:, VLAM:VLAM + 4], AF.Exp, scale=-1.0)
                ACT(cvec[l][:, 0:4], cvec[l][:, 0:4], AF.Ln, bias=1.0, scale=1.0)
                TS("dve", cvec[l][:, 4:8], cvec[l][:, 0:4], -16.0, None, ALU.mult)
                TS("dve", cvec[l][:, 0:4], cvec[l][:, 0:4], -8.0, None, ALU.mult)
                for wi, dst in enumerate((Wr[l], Wi[l])):
                    CP("dve", dst, bd_fs[2 * l + wi])
                b = nb()
                for h in range(4):
                    TR(PS[:, b, h * 128:(h + 1) * 128], wsls[l][:, h, :], ident, inc=(h == 3))
                CP("dve", wst_fs[l], PS[:, b, :].rearrange("p (h t) -> p h t", h=4))
                S.add("pool", lambda e, l=l: e.affine_select(out=WsT[l], in_=wst_fs[l], compare_op=ALU.is_ge, fill=0.0,
                                                             base=0, pattern=[[0, 4], [1, 128]], channel_multiplier=-1),
                      reads=[wst_fs[l]], writes=[WsT[l]])

        def load_kT_from_tok(l, tok):
            for mc in range(2):
                for g in range(2):
                    b = nb()
                    for c4 in range(4):
                        c = g * 4 + c4
                        TR(PS[:, b, c4 * 128:(c4 + 1) * 128], tok[:, mc, c * 128:(c + 1) * 128], ident, inc=(c4 == 3))
                    CP("dve", kT[l][:, g * 4:(g + 1) * 4, mc * 128:(mc + 1) * 128],
                       PS[:, b, :].rearrange("p (c m) -> p c m", c=4))

        def kv_prologue():
            for mc in range(2):
                for g in range(2):
                    b = nb()
                    for c4 in range(4):
                        c = g * 4 + c4
                        TR(PS[:, b, c4 * 128:(c4 + 1) * 128], memtok[:, mc, c * 128:(c + 1) * 128], ident, inc=(c4 == 3))
                    CP("dve", memT[:, g * 4:(g + 1) * 4, mc * 128:(mc + 1) * 128],
                       PS[:, b, :].rearrange("p (c m) -> p c m", c=4))

        def kv_layer(l):
            for i in range(2):
                w = use_piece(0, l, pidx[("wk", i)])
                for jj in range(4):
                    b = nb()
                    for k in range(8):
                        MM(PS[:, b, 0:256], w[:, k, jj * 128:(jj + 1) * 128], memT[:, k, :], k == 0, k == 7)
                    CP("act", kT[l][:, i * 4 + jj, :], PS[:, b, 0:256])
                for mc in range(2):
                    b = nb()
                    for k in range(8):
                        MM(PS[:, b, :], memT[:, k, mc * 128:(mc + 1) * 128], w[:, k, :], k == 0, k == 7)
                    CP("dve", kvh[mc], PS[:, b, :])
                    DMA("sp", dout["p_mk"][l, mc * 128:(mc + 1) * 128, i * 512:(i + 1) * 512], kvh[mc], reads=[kvh[mc]])
            for i in range(2):
                w = use_piece(0, l, pidx[("wv", i)])
                for mc in range(2):
                    b = nb()
                    for k in range(8):
                        MM(PS[:, b, :], memT[:, k, mc * 128:(mc + 1) * 128], w[:, k, :], k == 0, k == 7)
                    CP("dve", kvh[mc], PS[:, b, :])
                    DMA("sp", dout["p_mv"][l, mc * 128:(mc + 1) * 128, i * 512:(i + 1) * 512], kvh[mc], reads=[kvh[mc]])
                    CP("act", vv[l][:, mc, i * 512:(i + 1) * 512], kvh[mc])

        class Ctx:
            pass

        def norm_tail(cx, b, gcol, Vsrc, out):
            N = cx.N
            ACT(lnb[:, :N], PS[:, b, :N], AF.Ln, bias=epsb[:, 0:1], scale=1.0)
            ACT(rstd[:, :N], lnb[:, :N], AF.Exp, scale=-0.5)
            for c in range(8):
                STT(out[:, c, :N], xfm[:, c, :N], Vsrc[:, gcol + c:gcol + c + 1], rstd[:, :N], ALU.mult, ALU.mult)

        def norm(cx, gcol, Vsrc, out):
            N = cx.N
            b = nb()
            for c in range(8):
                MM(PS[:, b, :N], ones_div, sq[:, c, :N], c == 0, c == 7)
            norm_tail(cx, b, gcol, Vsrc, out)

        def mm_block(banks, lhs, rhs, korder, N, M=128):
            for ki, k in enumerate(korder):
                for jj, b in enumerate(banks):
                    MM(PS[:, b, :N], lhs(k, jj), rhs(k), ki == 0, ki == len(korder) - 1)

        def proj_resid(cx, l, kind, rhs_buf, kc, nrm, korder=None):
            N = cx.N
            bs = nb()
            bank_resv.add(bs)
            if kind == "wdn":
                w0 = use_piece(cx.gi, l, cx.pidx[(kind, 0)])
                w1 = use_piece(cx.gi, l, cx.pidx[(kind, 1)], hold=1)
                b01 = [nb(), nb()]
                for k in range(kc):
                    MM(PS[:, b01[0], :N], w0[:, k, :], rhs_buf[:, k, :N], k == 0, k == kc - 1)
                    MM(PS[:, b01[1], :N], w1[:, k, :], rhs_buf[:, k, :N], k == 0, k == kc - 1)
                for j in range(2):
                    TT("dve", xfm[:, j, :N], xfm[:, j, :N], PS[:, b01[j], :N], ALU.add)
                    ACT(sq[:, j, :N], xfm[:, j, :N], AF.Square)
                for j in range(2, 8):
                    w = use_piece(cx.gi, l, cx.pidx[(kind, j)])
                    b = nb()
                    for k in range(kc):
                        MM(PS[:, b, :N], w[:, k, :], rhs_buf[:, k, :N], k == 0, k == kc - 1)
                    TT("dve", xfm[:, j, :N], xfm[:, j, :N], PS[:, b, :N], ALU.add)
                    ACT(sq[:, j, :N], xfm[:, j, :N], AF.Square)
                    if j == 2:
                        MM(PS[:, bs, :N], ones_div, sq[:, 0, :N], True, False)
                    MM(PS[:, bs, :N], ones_div, sq[:, j - 1, :N], False, False)
                MM(PS[:, bs, :N], ones_div, sq[:, 7, :N], False, True)
            else:
                korder = korder or list(range(kc))
                for i in range(2):
                    w = use_piece(cx.gi, l, cx.pidx[(kind, i)])
                    banks = [nb() for _ in range(4)]
                    mm_block(banks, lambda k, jj: w[:, k, jj * 128:(jj + 1) * 128], lambda k: rhs_buf[:, k, :N],
                             korder, N)
                    for jj, b in enumerate(banks):
                        j = i * 4 + jj
                        TT("dve", xfm[:, j, :N], xfm[:, j, :N], PS[:, b, :N], ALU.add)
                        ACT(sq[:, j, :N], xfm[:, j, :N], AF.Square)
                    if i == 1:
                        for j in range(4):
                            MM(PS[:, bs, :N], ones_div, sq[:, j, :N], j == 0, False)
                for j in range(4, 8):
                    MM(PS[:, bs, :N], ones_div, sq[:, j, :N], False, j == 7)
            bank_resv.discard(bs)
            norm_tail(cx, bs, *nrm)

        def mixer(cx, l):
            N, TB, NTB, st = cx.N, cx.TB, cx.NTB, cx.st
            sb = l * 64
            Vl = V[l]
            xa_st = st[:, sb + SCA:sb + SCA + 12].rearrange("p (t j) -> p j t", t=3)
            h_st = st[:, sb + SH:sb + SH + 4]
            t_st = st[:, sb + SCB:sb + SCB + 4].rearrange("p (t c) -> p c t", t=2)
            CP("pool", xa_raw[:, :, 0:3], xa_st)
            CP("pool", t_raw[:, :, 0:2], t_st)

            def zchunk(w, lc):
                b = nb()
                for k in range(8):
                    MM(PS[:, b, :N], w[:, k, lc:lc + 128], xn[:, k, :N], k == 0, k == 7)
                return b

            w = use_piece(cx.gi, l, cx.pidx[("win", 0)])
            xa_banks = [nb() for _ in range(4)]
            mm_block(xa_banks, lambda k, jj: w[:, k, jj * 128:(jj + 1) * 128], lambda k: xn[:, k, :N], range(8), N)
            for j in range(4):
                b = xa_banks[j]
                ACT(xa_raw[:, j, 3:3 + N], PS[:, b, :N], AF.Identity)
                ACT(TA[:, j, :N], PS[:, b, :N], AF.Identity, bias=Vl[:, VCAB + j:VCAB + j + 1],
                    scale=Vl[:, VCAW + 12 + j:VCAW + 13 + j])
            CP("pool", xa_st, xa_raw[:, :, N:N + 3])
            for j in range(4):
                for k in range(3):
                    dst = xa_bf[:, j, :N] if k == 2 else TA[:, j, :N]
                    STT(dst, xa_raw[:, j, k:k + N], Vl[:, VCAW + 4 * k + j:VCAW + 4 * k + j + 1], TA[:, j, :N],
                        ALU.mult, ALU.add)
            w = use_piece(cx.gi, l, cx.pidx[("win", 1)])
            for j in range(4):
                b = zchunk(w, j * 128)
                ACT(yq[:, j, :N], PS[:, b, :N], AF.Gelu_apprx_tanh)
            w = use_piece(cx.gi, l, cx.pidx[("win", 2)])
            for c in range(2):
                b = zchunk(w, c * 128)
                ACT(yq[:, 6 + c, :N], PS[:, b, :N], AF.Gelu_apprx_tanh)
            for tb in range(NTB):
                b = nb()
                for k in range(8):
                    MM(PS[0:TB, b, 0:256], xn[:, k, tb * TB:(tb + 1) * TB], w[:, k, 256:512], k == 0, k == 7)
                ACT(vt[0:TB, tb, :], PS[0:TB, b, 0:256], AF.Gelu_apprx_tanh)
            for tb in range(NTB):
                TT("dve", vsq[0:TB, tb, :], vt[0:TB, tb, :], vt[0:TB, tb, :], ALU.mult)
                S.add("dve", lambda e, tb=tb: e.tensor_reduce(out=ssb[0:TB, tb, :],
                                                              in_=vsq[0:TB, tb, :].rearrange("p (g c) -> p g c", g=4),
                                                              axis=AX.X, op=ALU.add),
                      reads=[vsq[0:TB, tb, :]], writes=[ssb[0:TB, tb, :]])
            for j in range(4):
                b = nb()
                MM(PS[:, b, :N], Wr[l][:, j, :], xa_bf[:, j, :N], True, True)
                ACT(T1[:, j, :N], PS[:, b, :N], AF.Sigmoid, bias=Vl[:, VBR + j:VBR + j + 1], scale=1.0)
            for j in range(4):
                b = nb()
                MM(PS[:, b, :N], Wi[l][:, j, :], xa_bf[:, j, :N], True, True)
                ACT(T2[:, j, :N], PS[:, b, :N], AF.Sigmoid, bias=Vl[:, VBI + j:VBI + j + 1], scale=1.0)
            ssv = ssb[0:TB, 0:NTB, :]
            rsv = rsb[0:TB, 0:NTB, :]
            w = use_piece(cx.gi, l, cx.pidx[("win", 3)])
            for c in range(2):
                b = zchunk(w, c * 128)
                CP("act", xb_sb[:, c, :N], PS[:, b, :N])
            ACT(rsv, ssv, AF.Ln, bias=epsb[0:TB, 0:1], scale=1.0 / 64)
            ACT(rsv, rsv, AF.Exp, scale=-0.5)
            for tb in range(NTB):
                TT("dve", vsq[0:TB, tb, :].rearrange("p (g c) -> p g c", g=4),
                   vt[0:TB, tb, :].rearrange("p (g c) -> p g c", g=4),
                   rsb[0:TB, tb, :].unsqueeze(2).to_broadcast([TB, 4, 64]), ALU.mult)
                if cx.is_sample:
                    TT("dve", vn32[0:TB, :], vsq[0:TB, tb, :], gvrep[l][0:TB, :], ALU.mult)
                    CP("dve", vn_bf[0:TB, tb, :], vn32[0:TB, :])
                    DMA("pool", dout["s_vc"][l], vn32[0:TB, :], reads=[vn32[0:TB, :]])
                else:
                    TT("dve", vn_bf[0:TB, tb, :], vsq[0:TB, tb, :], gvrep[l][0:TB, :], ALU.mult)
            for c in range(2):
                b = zchunk(w, 256 + c * 128)
                CP("dve", yq[:, 4 + c, :N], PS[:, b, :N])
            for j in range(4):
                TT("pool", TA[:, j, :N], T2[:, j, :N], xa_bf[:, j, :N], ALU.mult)
            w = use_piece(cx.gi, l, cx.pidx[("win", 4)])
            for c in range(2):
                b = zchunk(w, c * 128)
                TT("dve", t_raw[:, c, 2:2 + N], xb_sb[:, c, :N], PS[:, b, :N], ALU.mult)
            CP("pool", t_st, t_raw[:, :, N:N + 2])
            mb = [nb(), nb()]
            for hp in range(2):
                for tb in range(NTB):
                    for hh in range(2):
                        h = 2 * hp + hh
                        MM(PS[hh * 64:(hh + 1) * 64, mb[hp], tb * TB:(tb + 1) * TB],
                           vn_bf[0:TB, tb, h * 64:(h + 1) * 64], WsT[l][0:TB, h, 0:TB], True, True,
                           tp=(0, hh * 64), wr=PS[:, mb[hp], tb * TB:(tb + 1) * TB])
            for c in range(2):
                TS("dve", zb[:, c, :N], t_raw[:, c, 2:2 + N], Vl[:, VCBW + 4 + c:VCBW + 5 + c], None, ALU.mult)
                for k in range(2):
                    dst = zb_bf[:, c, :N] if k == 1 else zb[:, c, :N]
                    STT(dst, t_raw[:, c, k:k + N], Vl[:, VCBW + 2 * k + c:VCBW + 2 * k + c + 1], zb[:, c, :N],
                        ALU.mult, ALU.add)
            for c in range(2):
                TT("dve", yq[:, 4 + c, :N], yq[:, 4 + c, :N], zb_bf[:, c, :N], ALU.mult)
            for hp in range(2):
                TT("dve", tmpc[:, hp, :N].rearrange("p (b t) -> p b t", b=NTB),
                   PS[:, mb[hp], :N].rearrange("p (b t) -> p b t", b=NTB),
                   biasbc[l][:, hp, 0:TB].unsqueeze(1).to_broadcast([128, NTB, TB]), ALU.add)
                TT("dve", yq[:, 6 + hp, :N], yq[:, 6 + hp, :N], tmpc[:, hp, :N], ALU.mult)
            for jp in range(2):
                js = (2 * jp, 2 * jp + 1)
                for j in js:
                    ACT(T3[:, j, :N], T1[:, j, :N], AF.Exp, scale=cvec[l][:, j:j + 1])
                for j in js:
                    ACT(T4[:, j, :N], T1[:, j, :N], AF.Exp, scale=cvec[l][:, 4 + j:5 + j])
                for j in js:
                    ACT(T4[:, j, :N], T4[:, j, :N], AF.Ln, bias=onep[:, 0:1], scale=-1.0)
                for j in js:
                    ACT(T4[:, j, :N], T4[:, j, :N], AF.Exp, scale=0.5)
                for j in js:
                    TT("dve", T2[:, j, :N], T4[:, j, :N], TA[:, j, :N], ALU.mult)
                for j in js:
                    S.add("dve", lambda e, j=j: e.tensor_tensor_scan(out=T1[:, j, :N], data0=T3[:, j, :N],
                                                                     data1=T2[:, j, :N], initial=h_st[:, j:j + 1],
                                                                     op0=ALU.mult, op1=ALU.add),
                          reads=[T3[:, j, :N], T2[:, j, :N], h_st[:, j:j + 1]], writes=[T1[:, j, :N]])
                    TT("dve", yq[:, j, :N], yq[:, j, :N], T1[:, j, :N], ALU.mult)
            CP("pool", h_st, T1[:, :, N - 1])
            proj_resid(cx, l, "wout", yq, 8, (VG_X, V[l], xn), korder=[4, 5, 6, 7, 0, 1, 2, 3])

        def attention(cx, l):
            N = cx.N
            for i in range(2):
                w = use_piece(cx.gi, l, cx.pidx[("wq", i)])
                if i == 0:
                    banks = [nb() for _ in range(4)]
                    mm_block(banks, lambda k, jj: w[:, k, jj * 128:(jj + 1) * 128], lambda k: xn[:, k, :N], range(8), N)
                    for jj, b in enumerate(banks):
                        ACT(yq[:, jj, :N], PS[:, b, :N], AF.Identity)
                    continue
                for jj in range(4):
                    b = nb()
                    for k in range(8):
                        MM(PS[:, b, :N], w[:, k, jj * 128:(jj + 1) * 128], xn[:, k, :N], k == 0, k == 7)
                    ACT(yq[:, i * 4 + jj, :N], PS[:, b, :N], AF.Identity)
            def scores(h):
                for mc in range(2):
                    b = nb()
                    for hc in range(2):
                        MM(PS[:, b, :N], kT[l][:, 2 * h + hc, mc * 128:(mc + 1) * 128], yq[:, 2 * h + hc, :N],
                           hc == 0, hc == 1)
                    ACT(pT[:, h, mc, :N], PS[:, b, :N], AF.Exp, scale=1.0 / 16)

            scores(0)
            for h in range(4):
                if h + 1 < 4:
                    scores(h + 1)
                b = nb()
                for mc in range(2):
                    MM(PS[:, b, :N], ones1, pT[:, h, mc, :N], mc == 0, mc == 1)
                rc = recip[h % 2]
                ACT(rc[:, :N], PS[:, b, :N], AF.Ln)
                ACT(rc[:, :N], rc[:, :N], AF.Exp, scale=-1.0)
                for hc in range(2):
                    b = nb()
                    for mc in range(2):
                        MM(PS[:, b, :N], vv[l][:, mc, h * 256 + hc * 128:h * 256 + (hc + 1) * 128], pT[:, h, mc, :N],
                           mc == 0, mc == 1)
                    TT("dve", ob[:, 2 * h + hc, :N], PS[:, b, :N], rc[:, :N], ALU.mult)
            proj_resid(cx, l, "wo", ob, 8, (VG_FFN, V[l], xn))

        def ffn(cx, l):
            N, st = cx.N, cx.st
            sb = l * 64
            Vl = V[l]
            g_st = st[:, sb + SCF:sb + SCF + 44].rearrange("p (t f) -> p f t", t=2)
            w = None
            for f in range(NFF):
                if f % 2 == 0:
                    w = use_piece(cx.gi, l, cx.pidx[("wup", f // 2)])
                lc = (f % 2) * 128
                r = f % 3
                if f == 0:
                    fb = [nb() for _ in range(4)]
                    mm_block(fb, lambda k, jj: w[:, k, jj % 2, (jj // 2) * 128:(jj // 2 + 1) * 128],
                             lambda k: xn[:, k, :N], range(8), N)
                if f < 2:
                    bg, bu = fb[2 * f], fb[2 * f + 1]
                else:
                    bg = nb()
                    for k in range(8):
                        MM(PS[:, bg, :N], w[:, k, 0, lc:lc + 128], xn[:, k, :N], k == 0, k == 7)
                    bu = nb()
                    for k in range(8):
                        MM(PS[:, bu, :N], w[:, k, 1, lc:lc + 128], xn[:, k, :N], k == 0, k == 7)
                gr = g_raw[r]
                CP("pool", gr[:, 0:2], g_st[:, f, :])
                ACT(gr[:, 2:2 + N], PS[:, bg, :N], AF.Identity)
                ACT(accb[r][:, :N], PS[:, bg, :N], AF.Identity, scale=Vl[:, VCFW + 44 + f:VCFW + 45 + f])
                CP("pool", g_st[:, f, :], gr[:, N:N + 2])
                STT(accb[r][:, :N], gr[:, 1:1 + N], Vl[:, VCFW + 22 + f:VCFW + 23 + f], accb[r][:, :N], ALU.mult, ALU.add)
                STT(accb[r][:, :N], gr[:, 0:N], Vl[:, VCFW + f:VCFW + f + 1], accb[r][:, :N], ALU.mult, ALU.add)
                ACT(slb[r][:, :N], accb[r][:, :N], AF.Silu)
                TT("dve", hbuf[:, f, :N], slb[r][:, :N], PS[:, bu, :N], ALU.mult)
                if f == NFF - 1:
                    ACT(epsb[:, 1:2], epsb[:, 0:1], AF.Ln)
            proj_resid(cx, l, "wdn", hbuf, NFF, (VG_MIX, V[1], xn) if l == 0 else (0, Vf, yfin))

        x_issued = set()

        def xbuf(cx, tb):
            if cx.gi == 0:
                return (xtok + ytok)[tb % 4]
            return xtok[tb % 2]

        def issue_x(cx, tb):
            key = (cx.gi, tb)
            if key in x_issued or tb >= cx.NTB:
                return
            x_issued.add(key)
            TB = cx.TB
            xt = xbuf(cx, tb)
            DMA("pool", xt[0:TB, :], cx.xsrc[cx.t0 + tb * TB:cx.t0 + (tb + 1) * TB, :], writes=[xt[0:TB, :]])

        def load_x(cx, nxt=None):
            N, TB, NTB = cx.N, cx.TB, cx.NTB
            issue_x(cx, 0)
            issue_x(cx, 1)
            for tb in range(NTB):
                xt = xbuf(cx, tb)
                for g in range(2):
                    b = nb()
                    for c4 in range(4):
                        c = g * 4 + c4
                        TR(PS[:, b, c4 * TB:(c4 + 1) * TB], xt[0:TB, c * 128:(c + 1) * 128], ident[0:TB, 0:TB],
                           inc=(c4 == 3))
                    src = PS[:, b, 0:4 * TB].rearrange("p (c t) -> p c t", c=4)
                    CP("dve", xfm[:, g * 4:(g + 1) * 4, tb * TB:(tb + 1) * TB], src)
                    ACT(sq[:, g * 4:(g + 1) * 4, tb * TB:(tb + 1) * TB], src, AF.Square)
                if tb + 2 < NTB:
                    issue_x(cx, tb + 2)
                elif nxt is not None:
                    issue_x(nxt, tb + 2 - NTB)

        def store_y(cx):
            N, TB, NTB = cx.N, cx.TB, cx.NTB
            for tb in range(NTB):
                yt = ytok[tb % 2]
                for g in range(2):
                    b = nb()
                    for c4 in range(4):
                        c = g * 4 + c4
                        TR(PS[0:TB, b, c4 * 128:(c4 + 1) * 128], yfin[:, c, tb * TB:(tb + 1) * TB], ident, inc=(c4 == 3))
                    CP("act" if g == 0 else "dve", yt[0:TB, g * 512:(g + 1) * 512], PS[0:TB, b, :])
                DMA("pool", cx.ydst[cx.t0 + tb * TB:cx.t0 + (tb + 1) * TB, :], yt[0:TB, :], reads=[yt[0:TB, :]])

        def store_states(st, names):
            b = nb()
            TR(PS[:, b, 0:128], st, ident)
            CP("dve", stage, PS[:, b, 0:128])
            for l in range(2):
                r0 = l * 64
                DMA("pool", dout[names[0]][l].rearrange("t (c p) -> (t c) p", p=128), stage[r0 + SCA:r0 + SCA + 12, :],
                    reads=[stage])
                DMA("pool", dout[names[1]][l].rearrange("(c p) -> c p", p=128), stage[r0 + SH:r0 + SH + 4, :],
                    reads=[stage])
                DMA("pool", dout[names[2]][l].rearrange("t (c p) -> (t c) p", p=128), stage[r0 + SCB:r0 + SCB + 4, :],
                    reads=[stage])
                DMA("pool", dout[names[3]][l].rearrange("t (c p) -> (t c) p", p=128), stage[r0 + SCF:r0 + SCF + 44, :],
                    reads=[stage])

        pidx = {(kind, i): n for n, (kind, i, kc, ncol) in enumerate(PIECES_X)}
        ctxs = []
        for gi, (gk, ti) in enumerate(groups):
            cx = Ctx()
            cx.gi = gi
            cx.gk, cx.ti = gk, ti
            cx.pidx = pidx
            cx.is_sample = gk == "s"
            if gk == "p":
                cx.N, cx.TB, cx.NTB = NT, 128, NT // 128
                cx.xsrc, cx.ydst, cx.t0, cx.st = din["xp"], dout["yp"], ti * NT, ST[0]
            else:
                cx.N, cx.TB, cx.NTB = DEC, DEC, 1
                cx.xsrc, cx.ydst, cx.t0, cx.st = din["xs"], dout["ys"], 0, ST[1]
            ctxs.append(cx)
        import os
        if os.environ.get("KV_FIRST"):
            kv_prologue()
        setup_consts_dma()
        for tb in range(4):
            issue_x(ctxs[0], tb)
        wstate["defer"] = []
        while wstate["loaded"] < min(len(seq), NSLOT):
            record_load(wstate["loaded"])
            wstate["loaded"] += 1
        setup_consts_compute()
        for wb in wstate["defer"]:
            wb()
        wstate["defer"] = None
        if not os.environ.get("KV_FIRST"):
            kv_prologue()
        for gi, cx in enumerate(ctxs):
            gk, ti = cx.gk, cx.ti
            nxt = ctxs[gi + 1] if gi + 1 < len(ctxs) else None
            if gk == "s":
                for l in range(2):
                    DMA("sp", memtok, din["ck"][l].rearrange("(m p) f -> p m f", p=128), writes=[memtok])
                    load_kT_from_tok(l, memtok)
                    DMA("pool", vv[l], din["cv"][l].rearrange("(m p) f -> p m f", p=128), writes=[vv[l]])
            if gi == 0:
                load_x(cx, nxt)
                norm(cx, VG_MIX, V[0], xn)
            for l in range(2):
                mixer(cx, l)
                if gi == 0:
                    kv_layer(l)
                attention(cx, l)
                ffn(cx, l)
            if nxt is not None:
                load_x(nxt, ctxs[gi + 2] if gi + 2 < len(ctxs) else None)
                norm(nxt, VG_MIX, V[0], xn)
            store_y(cx)
            if gk == "p" and ti == n_ptiles - 1:
                store_states(ST[0], ["p_ca", "p_h", "p_cb", "p_cf"])
            if gk == "s":
                store_states(ST[1], ["s_ca", "s_h", "s_cb", "s_cf"])

        with nc.Block() as block:
            S.emit(block)
    return nc


_OUT_ORDER = ["yp", "ys", "p_ca", "p_h", "p_cb", "p_cf", "p_mk", "p_mv", "s_ca", "s_h", "s_cb", "s_cf", "s_vc"]


def kernel(x_prompt, x_sample, mem_prompt, cache_mem_k, cache_mem_v, state_conv_a, state_h_a, state_conv_b,
           state_conv_ffn, **weights):
    n = 8
    f = lambda a: np.ascontiguousarray(np.asarray(a, dtype=np.float32))
    wmap = {k: f(weights[k]) for k in W_NAMES}
    in_maps = []
    for b in range(n):
        m = dict(wmap)
        m["xp"] = f(x_prompt[b])
        m["xs"] = f(x_sample[b])
        m["mem"] = f(mem_prompt[b])
        m["ck"] = f(np.asarray(cache_mem_k)[:, b].reshape(2, NMEM, D))
        m["cv"] = f(np.asarray(cache_mem_v)[:, b].reshape(2, NMEM, D))
        m["sca"] = f(np.asarray(state_conv_a)[:, b])
        m["sh"] = f(np.asarray(state_h_a)[:, b])
        m["scb"] = f(np.asarray(state_conv_b)[:, b])
        m["scf"] = f(np.asarray(state_conv_ffn)[:, b])
        in_maps.append(m)
    nc = build_program()
    res = run_bass_kernel_spmd(nc, in_maps, core_ids=list(range(n)))
    r = res.results
    outs = {}
    outs["yp"] = np.stack([r[b]["yp"] for b in range(n)], 0)
    outs["ys"] = np.stack([r[b]["ys"] for b in range(n)], 0)
    for k in ["p_ca", "p_h", "p_cb", "p_cf", "s_ca", "s_h", "s_cb", "s_cf", "s_vc"]:
        outs[k] = np.stack([r[b][k] for b in range(n)], 1)
    for k in ["p_mk", "p_mv"]:
        outs[k] = np.stack([r[b][k] for b in range(n)], 1).reshape(2, n, NMEM, 4, 256)
    return tuple(np.ascontiguousarray(outs[k], dtype=np.float32) for k in _OUT_ORDER)
```

```python
import numpy as np
from contextlib import ExitStack
import concourse.bass as bass
import concourse.mybir as mybir
from concourse.bass_utils import run_bass_kernel_spmd

F32 = mybir.dt.float32
BF16 = mybir.dt.bfloat16
ALU = mybir.AluOpType
AF = mybir.ActivationFunctionType
AX = mybir.AxisListType

D = 1024
SEQ = 4096
NT = 512
DEC = 32
NMEM = 256
DFF = 2816
NFF = 22
EPS = 1e-6
NSLOT = 4
SLOTW = 2048
GRAN = 512
_ISZ = {F32: 4, BF16: 2}


PSUM_BASE = 10_000_000
SMALL_END = [0]
GRAN_S = 16


def ap_keys(ap):
    space = str(ap.space).upper()
    sb = "SB" in space
    isz = _ISZ[ap.dtype]
    dims = ap.ap
    pstep = dims[0][0]
    off = ap.offset % pstep if pstep > 0 else ap.offset
    free = list(dims[1:]) or [(1, 1)]
    last_step, last_cnt = free[-1]
    run = (last_cnt - 1) * abs(last_step) + 1
    outer = free[:-1]
    keys = set()
    starts = [off]
    for step, cnt in outer:
        if step == 0 or cnt == 1:
            continue
        starts = [s + i * step for s in starts for i in range(cnt)]
    if sb and off * isz < SMALL_END[0]:
        base, gran = 20_000_000, GRAN_S
    elif sb:
        base, gran = 0, GRAN
    else:
        base, gran = PSUM_BASE, 2048
    for s in starts:
        lo = s * isz
        hi = (s + run) * isz
        for g in range(lo // gran, (hi - 1) // gran + 1):
            keys.add(base + g)
    return keys


class Op:
    __slots__ = ("fn", "waits", "inc", "sem", "val", "dma")


class Sched:
    ENGS = ("pe", "act", "dve", "pool", "sp")

    def __init__(self, nc, es):
        self.nc = nc
        n_dma = {"sp": 24, "pool": 24, "act": 8}
        self.sems = []

        def new_sem(name):
            self.sems.append(es.enter_context(nc.semaphore(name)))
            return len(self.sems) - 1

        self.eng_sem = {e: new_sem("s_" + e) for e in self.ENGS}
        self.dma_pool = {e: [new_sem(f"d_{e}{i}") for i in range(n)] for e, n in n_dma.items()}
        self.dma_rr = {e: 0 for e in self.dma_pool}
        self.dma_val = {}
        self.ops = {e: [] for e in self.ENGS}
        self.count = {e: 0 for e in self.ENGS}
        self.last_inc = {e: True for e in self.ENGS}
        self.last_w = {}
        self.readers = {}
        self.seen = {e: {} for e in self.ENGS}

    def add(self, eng, fn, reads=(), writes=(), rkeys=(), wkeys=(), inc=True, dma=False):
        kr = set(rkeys)
        for ap in reads:
            kr |= ap_keys(ap)
        kw = set(wkeys)
        for ap in writes:
            kw |= ap_keys(ap)
        op = Op()
        op.fn = fn
        op.dma = dma
        op.inc = inc
        if dma:
            pool = self.dma_pool[eng]
            own = pool[self.dma_rr[eng] % len(pool)]
            self.dma_rr[eng] += 1
        else:
            own = self.eng_sem[eng]
        deps = {}
        last_w = self.last_w
        readers = self.readers
        for k in kr:
            lw = last_w.get(k)
            if lw is not None:
                s, v = lw
                if (s != own or eng != "pe") and deps.get(s, 0) < v:
                    deps[s] = v
            if isinstance(k, int) and PSUM_BASE <= k < 2 * PSUM_BASE:
                rd = readers.get(k)
                if rd:
                    for s, v in rd.items():
                        if s != own and deps.get(s, 0) < v:
                            deps[s] = v
        for k in kw:
            lw = last_w.get(k)
            if lw is not None:
                s, v = lw
                if s != own and deps.get(s, 0) < v:
                    deps[s] = v
            rd = readers.get(k)
            if rd:
                for s, v in rd.items():
                    if s != own and deps.get(s, 0) < v:
                        deps[s] = v
        if dma:
            prev = self.dma_val.get(own, 0)
            if prev and deps.get(own, 0) < prev:
                deps[own] = prev
        seen = self.seen[eng]
        waits = []
        for s, v in deps.items():
            if seen.get(s, 0) < v:
                waits.append((s, v))
                seen[s] = v
        op.waits = waits
        if dma:
            val = self.dma_val.get(own, 0) + 16
            self.dma_val[own] = val
        elif inc:
            self.count[eng] += 1
            val = self.count[eng]
            self.last_inc[eng] = True
        else:
            val = self.count[eng] + 1
            self.last_inc[eng] = False
        op.sem = own
        op.val = val
        for k in kw:
            last_w[k] = (own, val)
            readers[k] = {}
        for k in kr:
            rd = readers.get(k)
            if rd is None:
                rd = readers[k] = {}
            if rd.get(own, 0) < val:
                rd[own] = val
        self.ops[eng].append(op)
        return op

    def emit(self, block):
        for e in self.ENGS:
            assert self.last_inc[e], f"engine {e} ends with non-inc op"
        sems = self.sems

        def run(ename, eng):
            own = sems[self.eng_sem[ename]]
            for op in self.ops[ename]:
                for s, v in op.waits:
                    eng.wait_ge(sems[s], v)
                ins = op.fn(eng)
                if op.dma:
                    ins.then_inc(sems[op.sem], 16)
                elif op.inc:
                    ins.then_inc(own, 1)
            for s in self.dma_pool.get(ename, []):
                v = self.dma_val.get(s, 0)
                if v:
                    eng.wait_ge(sems[s], v)

        block.tensor(lambda eng: run("pe", eng))
        block.scalar(lambda eng: run("act", eng))
        block.vector(lambda eng: run("dve", eng))
        block.gpsimd(lambda eng: run("pool", eng))
        block.sync(lambda eng: run("sp", eng))


W_NAMES = ["g_mix", "w_in", "conv_a_w", "conv_a_b", "w_rg", "b_rg", "w_ig", "b_ig", "lam", "conv_b_w", "g_v",
           "w_s", "b_s", "w_out", "g_x", "w_q", "w_k", "w_v", "w_o", "g_ffn", "w_up", "conv_f_w", "w_down", "g_final"]
W_SHAPES = {
    "g_mix": [2, D], "w_in": [2, D, 2304], "conv_a_w": [2, 4, 512], "conv_a_b": [2, 512], "w_rg": [2, 8, 64, 64],
    "b_rg": [2, 512], "w_ig": [2, 8, 64, 64], "b_ig": [2, 512], "lam": [2, 512], "conv_b_w": [2, 3, 256],
    "g_v": [2, 256], "w_s": [2, 4, 128, 128], "b_s": [2, 4, 128], "w_out": [2, D, D], "g_x": [2, D],
    "w_q": [2, D, D], "w_k": [2, D, D], "w_v": [2, D, D], "w_o": [2, D, D], "g_ffn": [2, D],
    "w_up": [2, D, 2 * DFF], "conv_f_w": [2, 3, DFF], "w_down": [2, DFF, D], "g_final": [D],
}
IN_SHAPES = {
    "xp": [SEQ, D], "xs": [DEC, D], "mem": [NMEM, D], "ck": [2, NMEM, D], "cv": [2, NMEM, D],
    "sca": [2, 3, 512], "sh": [2, 512], "scb": [2, 2, 256], "scf": [2, 2, DFF],
}
OUT_SHAPES = {
    "yp": [SEQ, D], "ys": [DEC, D], "p_ca": [2, 3, 512], "p_h": [2, 512], "p_cb": [2, 2, 256], "p_cf": [2, 2, DFF],
    "p_mk": [2, NMEM, D], "p_mv": [2, NMEM, D], "s_ca": [2, 3, 512], "s_h": [2, 512], "s_cb": [2, 2, 256],
    "s_cf": [2, 2, DFF], "s_vc": [2, DEC, 256],
}

WIN_COLS = [(0, 512), (512, 512), (1792, 512), (1024, 512), (1536, 256)]
PIECES = ([("win", i, 8, WIN_COLS[i][1]) for i in range(5)] + [("wout", i, 8, 512) for i in range(2)]
          + [("wq", i, 8, 512) for i in range(2)] + [("wo", i, 8, 512) for i in range(2)]
          + [("wup", i, 8, 512) for i in range(11)] + [("wdn", i, 22, 128) for i in range(8)])
NPIECE = len(PIECES)
PIECES_X = PIECES + [("wk", 0, 8, 512), ("wk", 1, 8, 512), ("wv", 0, 8, 512), ("wv", 1, 8, 512)]


def build_program(n_ptiles=SEQ // NT, do_sample=True):
    nc = bass.Bass("TRN2", target_bir_lowering=False)
    din = {k: nc.dram_tensor(k, s, F32, kind="ExternalInput").ap() for k, s in IN_SHAPES.items()}
    dw = {k: nc.dram_tensor(k, s, F32, kind="ExternalInput").ap() for k, s in W_SHAPES.items()}
    dout = {k: nc.dram_tensor(k, s, F32, kind="ExternalOutput").ap() for k, s in OUT_SHAPES.items()}
    wscr = nc.dram_tensor("wscr", [2, NPIECE, 128, 2 * SLOTW], BF16).ap()

    with ExitStack() as es:
        AW = 52600
        A = es.enter_context(nc.sbuf_tensor("arena", [128, AW], F32))
        PS = es.enter_context(nc.psum_tensor("ps", [128, 8, 512], F32))
        S = Sched(nc, es)
        pos = [0]
        maxpos = [0]

        def f32v(off, words):
            return A[:, off:off + words]

        def bfv(off, words):
            return A[:, off:off + words].bitcast(BF16)

        def alloc(words):
            pos[0] = (pos[0] + 127) // 128 * 128
            o = pos[0]
            pos[0] += words
            assert pos[0] <= AW, pos[0]
            maxpos[0] = max(maxpos[0], pos[0])
            return o

        def salloc(words):
            o = pos[0]
            pos[0] += (words + 3) // 4 * 4
            return o

        ident = f32v(salloc(128), 128)
        ones_div = bfv(salloc(64), 64)
        ones1 = bfv(salloc(64), 64)
        epsb = f32v(salloc(4), 2)
        onep = f32v(salloc(4), 2)
        WsT = [bfv(salloc(256), 256).rearrange("p (h t) -> p h t", h=4) for _ in range(2)]
        biasbc = [f32v(salloc(256), 256).rearrange("p (h t) -> p h t", h=2) for _ in range(2)]
        Wr = [bfv(salloc(256), 256).rearrange("p (c j) -> p c j", c=4) for _ in range(2)]
        Wi = [bfv(salloc(256), 256).rearrange("p (c j) -> p c j", c=4) for _ in range(2)]
        gvrep = [f32v(salloc(256), 256) for _ in range(2)]
        V = [f32v(salloc(128), 128) for _ in range(2)]
        Vf = f32v(salloc(128), 128)
        cvec = [f32v(salloc(8), 8) for _ in range(2)]
        ST = [f32v(salloc(128), 128) for _ in range(2)]
        ssb = f32v(salloc(16), 16).rearrange("p (b g) -> p b g", b=4)
        rsb = f32v(salloc(16), 16).rearrange("p (b g) -> p b g", b=4)
        pos[0] = (pos[0] + 127) // 128 * 128
        SMALL_END[0] = pos[0] * 4
        o_xfm = alloc(8 * NT)
        xfm = f32v(o_xfm, 8 * NT).rearrange("p (c n) -> p c n", c=8)
        o_xn = alloc(4 * NT)
        xn = bfv(o_xn, 4 * NT).rearrange("p (c n) -> p c n", c=8)
        o_yq = alloc(4 * NT)
        yq = bfv(o_yq, 4 * NT).rearrange("p (c n) -> p c n", c=8)
        rstd = f32v(alloc(NT), NT)
        lnb = f32v(alloc(NT), NT)
        xtok = [f32v(alloc(D), D) for _ in range(2)]
        ytok = [f32v(alloc(D), D) for _ in range(2)]
        kT = [bfv(alloc(1024), 1024).rearrange("p (c m) -> p c m", c=8) for _ in range(2)]
        vv = [bfv(alloc(1024), 1024).rearrange("p (c f) -> p c f", c=2) for _ in range(2)]
        slots = [alloc(SLOTW) for _ in range(NSLOT)]
        memT = bfv(alloc(1024), 1024).rearrange("p (c m) -> p c m", c=8)
        kvh = [f32v(alloc(NT), NT) for _ in range(2)]
        R0 = pos[0]
        o_xaraw = alloc(4 * 640)
        xa_raw = f32v(o_xaraw, 4 * 640).rearrange("p (c n) -> p c n", c=4)
        TA, T1, T2, T3, T4 = [f32v(alloc(4 * NT), 4 * NT).rearrange("p (c n) -> p c n", c=4) for _ in range(5)]
        xa_bf = bfv(alloc(2 * NT), 2 * NT).rearrange("p (c n) -> p c n", c=4)
        o_xb = alloc(2 * NT)
        xb_sb = f32v(o_xb, 2 * NT).rearrange("p (c n) -> p c n", c=2)
        zb_bf = bfv(o_xb, NT).rearrange("p (c n) -> p c n", c=2)
        t_raw = f32v(alloc(2 * 640), 2 * 640).rearrange("p (c n) -> p c n", c=2)
        zb = f32v(alloc(2 * NT), 2 * NT).rearrange("p (c n) -> p c n", c=2)
        vt = f32v(alloc(1024), 1024).rearrange("p (b f) -> p b f", b=4)
        vsq = f32v(alloc(1024), 1024).rearrange("p (b f) -> p b f", b=4)
        vn_bf = bfv(alloc(512), 512).rearrange("p (b f) -> p b f", b=4)
        vn32 = f32v(alloc(256), 256)
        o_tmpc = alloc(2 * NT)
        tmpc = bfv(o_tmpc, NT).rearrange("p (c n) -> p c n", c=2)
        R_end_mixer = pos[0]
        pos[0] = R0
        sq = bfv(alloc(4 * NT), 4 * NT).rearrange("p (c n) -> p c n", c=8)
        pT = bfv(alloc(4 * NT), 4 * NT).rearrange("p (h m n) -> p h m n", h=4, m=2)
        ob = bfv(alloc(4 * NT), 4 * NT).rearrange("p (c n) -> p c n", c=8)
        recip = [f32v(alloc(NT), NT) for _ in range(2)]
        pos[0] = R0 + 4 * NT
        hbuf = bfv(alloc(11 * NT), 11 * NT).rearrange("p (f n) -> p f n", f=NFF)
        g_raw = [f32v(alloc(640), 640) for _ in range(3)]
        accb = [f32v(alloc(NT), NT) for _ in range(3)]
        slb = [f32v(alloc(NT), NT) for _ in range(3)]
        pos[0] = R0 + 4 * NT
        yfin = f32v(alloc(8 * NT), 8 * NT).rearrange("p (c n) -> p c n", c=8)
        pos[0] = R0
        memtok = f32v(alloc(2048), 2048).rearrange("p (m f) -> p m f", m=2)
        stage = f32v(alloc(128), 128)
        stages = [f32v(alloc(128), 128) for _ in range(4)]
        wsls = [f32v(alloc(512), 512).rearrange("p (h s) -> p h s", h=4) for _ in range(2)]
        wst_fs = [f32v(alloc(512), 512).rearrange("p (h t) -> p h t", h=4) for _ in range(2)]
        bd_fs = [f32v(alloc(512), 512).rearrange("p (c j) -> p c j", c=4) for _ in range(4)]
        print('SBUF words used', maxpos[0], 'of', AW)

        bank_ctr = [0]
        bank_resv = set()

        def nb():
            while True:
                b = bank_ctr[0] % 8
                bank_ctr[0] += 1
                if b not in bank_resv:
                    return b

        def is_ap(x):
            return not isinstance(x, (int, float)) and x is not None

        def ACT(out, in_, func, bias=None, scale=None):
            rd = [in_] + [x for x in (bias, scale) if is_ap(x)]
            kw = {}
            if bias is not None:
                kw["bias"] = bias
            if scale is not None:
                kw["scale"] = scale
            S.add("act", lambda e: e.activation(out=out, in_=in_, func=func, **kw), reads=rd, writes=[out])

        def TT(eng, out, in0, in1, op):
            S.add(eng, lambda e: e.tensor_tensor(out=out, in0=in0, in1=in1, op=op), reads=[in0, in1], writes=[out])

        def TS(eng, out, in0, s1, s2, op0, op1=None):
            rd = [in0] + [x for x in (s1, s2) if is_ap(x)]
            if op1 is None:
                S.add(eng, lambda e: e.tensor_scalar(out=out, in0=in0, scalar1=s1, scalar2=None, op0=op0),
                      reads=rd, writes=[out])
            else:
                S.add(eng, lambda e: e.tensor_scalar(out=out, in0=in0, scalar1=s1, scalar2=s2, op0=op0, op1=op1),
                      reads=rd, writes=[out])

        def STT(out, in0, scalar, in1, op0, op1):
            rd = [in0, in1] + ([scalar] if is_ap(scalar) else [])
            S.add("dve", lambda e: e.scalar_tensor_tensor(out=out, in0=in0, scalar=scalar, in1=in1, op0=op0, op1=op1),
                  reads=rd, writes=[out])

        def CP(eng, out, in_):
            if eng == "act":
                S.add(eng, lambda e: e.activation(out=out, in_=in_, func=AF.Identity), reads=[in_], writes=[out])
            else:
                S.add(eng, lambda e: e.tensor_copy(out=out, in_=in_), reads=[in_], writes=[out])

        def MM(out, lhsT, rhs, start, stop, tp=None, wr=None):
            kw = {}
            if tp is not None:
                kw["tile_position"] = tp
            S.add("pe", lambda e: e.matmul(out, lhsT=lhsT, rhs=rhs, start=start, stop=stop, **kw),
                  reads=[lhsT, rhs], writes=[wr if wr is not None else out], inc=stop)

        def TR(out, in_, idn, inc=True):
            S.add("pe", lambda e: e.transpose(out=out, in_=in_, identity=idn), reads=[in_, idn], writes=[out], inc=inc)

        def DMA(q, out, in_, reads=(), writes=(), rkeys=(), wkeys=(), slow=False):
            if slow:
                S.add(q, lambda e: e.dma_start(out=out, in_=in_, allow_slow_non_contiguous=True),
                      reads=reads, writes=writes, rkeys=rkeys, wkeys=wkeys, dma=True)
            else:
                S.add(q, lambda e: e.dma_start(out=out, in_=in_), reads=reads, writes=writes, rkeys=rkeys,
                      wkeys=wkeys, dma=True)

        def MEMSET(eng, ap, val):
            S.add(eng, lambda e: e.memset(ap, val), writes=[ap])

        groups = [("p", i) for i in range(n_ptiles)] + ([("s", 0)] if do_sample else [])
        base_order = list(range(NPIECE))
        first_order = base_order[:7] + [NPIECE, NPIECE + 1, NPIECE + 2, NPIECE + 3] + base_order[7:]
        seq = [(gi, l, pi) for gi in range(len(groups)) for l in range(2)
               for pi in (first_order if gi == 0 else base_order)]
        wstate = {"loaded": 0, "cur": 0}

        def src_aps(l, pi):
            kind, i, kc, ncol = PIECES_X[pi]
            if kind in ("wk", "wv"):
                nm = {"wk": "w_k", "wv": "w_v"}[kind]
                return [(dw[nm][l][:, i * 512:(i + 1) * 512].rearrange("(k p) n -> p k n", p=128), None)]
            if kind == "win":
                c0 = WIN_COLS[i][0]
                return [(dw["w_in"][l][:, c0:c0 + ncol].rearrange("(k p) n -> p k n", p=128), None)]
            if kind in ("wout", "wq", "wo"):
                nm = {"wout": "w_out", "wq": "w_q", "wo": "w_o"}[kind]
                return [(dw[nm][l][:, i * 512:(i + 1) * 512].rearrange("(k p) n -> p k n", p=128), None)]
            if kind == "wup":
                return [(dw["w_up"][l][:, i * 256:(i + 1) * 256].rearrange("(k p) n -> p k n", p=128), 0),
                        (dw["w_up"][l][:, DFF + i * 256:DFF + (i + 1) * 256].rearrange("(k p) n -> p k n", p=128), 1)]
            if kind == "wdn":
                return [(dw["w_down"][l][:, i * 128:(i + 1) * 128].rearrange("(k p) n -> p k n", p=128), None)]
            raise ValueError(kind)

        def slot_view(sidx, pi):
            kind, i, kc, ncol = PIECES_X[pi]
            nel = kc * ncol
            flat = bfv(slots[sidx], (nel + 1) // 2)
            if kind == "wup":
                return flat, flat.rearrange("p (k h n) -> p k h n", k=8, h=2)
            return flat, flat.rearrange("p (k n) -> p k n", k=kc)

        def record_load(idx):
            gi, l, pi = seq[idx]
            sidx = idx % NSLOT
            flat, view = slot_view(sidx, pi)
            kind, i, kc, ncol = PIECES_X[pi]
            nel = kc * ncol
            wb_tile = min(pi % 3, n_ptiles - 1)
            if gi <= wb_tile and groups[gi][0] == "p":
                for src, half in src_aps(l, pi):
                    dst = view if half is None else view[:, :, half, :]
                    DMA("pool", dst, src, writes=[dst])
                if pi < NPIECE and gi == wb_tile:
                    wb = lambda: DMA("sp", wscr[l, pi, :, 0:nel], flat, reads=[flat], wkeys=[("scr", l, pi)])
                    if wstate.get("defer") is not None:
                        wstate["defer"].append(wb)
                    else:
                        wb()
            else:
                DMA("sp", flat, wscr[l, pi, :, 0:nel], writes=[flat], rkeys=[("scr", l, pi)])

        def use_piece(gi, l, pi, hold=0):
            idx = wstate["cur"]
            assert seq[idx] == (gi, l, pi), (seq[idx], gi, l, pi)
            while wstate["loaded"] < min(len(seq), idx + NSLOT - hold):
                record_load(wstate["loaded"])
                wstate["loaded"] += 1
            wstate["cur"] += 1
            return slot_view(idx % NSLOT, pi)[1]

        MEMSET("pool", ident, 0.0)
        S.add("pool", lambda e: e.affine_select(out=ident, in_=ident, compare_op=ALU.not_equal, fill=1.0, base=0,
                                                pattern=[[-1, 128]], channel_multiplier=1),
              reads=[ident], writes=[ident])
        MEMSET("pool", ones_div, 1.0 / 1024)
        MEMSET("pool", ones1, 1.0)
        MEMSET("pool", epsb, EPS)
        MEMSET("pool", onep, 1.0 + 2.0 ** -23)
        MEMSET("pool", ST[0], 0.0)

        VG_MIX, VG_X, VG_FFN, VCAW, VCAB, VBR, VBI, VLAM, VCBW, VCFW = 0, 8, 16, 24, 40, 44, 48, 52, 56, 62
        SCA, SH, SCB, SCF = 0, 12, 16, 20

        const_state = {}

        def setup_consts_dma():
            stage_jobs = []
            cq_ctr = [0]

            def cq():
                cq_ctr[0] += 1
                return "sp" if cq_ctr[0] % 2 else "act"

            def stage_rows(rows, dst):
                stg = stages[len(stage_jobs)]
                r0 = 0
                for ap in rows:
                    n = ap.shape[0]
                    DMA(cq(), stg[r0:r0 + n, :], ap, writes=[stg[r0:r0 + n, :]])
                    r0 += n
                stage_jobs.append((stg, r0, dst))

            for l in range(2):
                rows = [dw["g_mix"][l].rearrange("(c p) -> c p", p=128), dw["g_x"][l].rearrange("(c p) -> c p", p=128),
                        dw["g_ffn"][l].rearrange("(c p) -> c p", p=128),
                        dw["conv_a_w"][l].rearrange("k (c p) -> (k c) p", p=128),
                        dw["conv_a_b"][l].rearrange("(c p) -> c p", p=128), dw["b_rg"][l].rearrange("(c p) -> c p", p=128),
                        dw["b_ig"][l].rearrange("(c p) -> c p", p=128), dw["lam"][l].rearrange("(c p) -> c p", p=128),
                        dw["conv_b_w"][l].rearrange("k (c p) -> (k c) p", p=128),
                        dw["conv_f_w"][l].rearrange("k (c p) -> (k c) p", p=128)]
                stage_rows(rows, V[l])
            stage_rows([dw["g_final"].rearrange("(c p) -> c p", p=128), dw["g_v"][0].rearrange("(c p) -> c p", p=128),
                        dw["g_v"][1].rearrange("(c p) -> c p", p=128)], Vf)
            if do_sample:
                rows = []
                for l in range(2):
                    rows += [din["sca"][l].rearrange("t (c p) -> (t c) p", p=128), din["sh"][l].rearrange("(c p) -> c p", p=128),
                             din["scb"][l].rearrange("t (c p) -> (t c) p", p=128),
                             din["scf"][l].rearrange("t (c p) -> (t c) p", p=128)]
                stage_rows(rows, ST[1])
            const_state["stage_jobs"] = stage_jobs
            for l in range(2):
                for wi, wname in enumerate(("w_rg", "w_ig")):
                    bd = bd_fs[2 * l + wi]
                    MEMSET("pool", bd, 0.0)
                    for hh in range(2):
                        src = dw[wname][l].rearrange("(c h) i j -> h i c j", h=2)[hh]
                        d = bd[hh * 64:(hh + 1) * 64, :, hh * 64:(hh + 1) * 64]
                        DMA(cq(), d, src, writes=[d])
                DMA(cq(), wsls[l], dw["w_s"][l].rearrange("h t s -> t h s"), writes=[wsls[l]])
                for hp in range(2):
                    for hh in range(2):
                        d = biasbc[l][hh * 64:(hh + 1) * 64, hp, :]
                        DMA(cq(), d, dw["b_s"][l][2 * hp + hh].partition_broadcast(64), writes=[d])
                DMA(cq(), gvrep[l], dw["g_v"][l].partition_broadcast(128), writes=[gvrep[l]])
            DMA("sp", memtok, din["mem"].rearrange("(m p) f -> p m f", p=128), writes=[memtok])

        def setup_consts_compute():
            for stg, r0, dst in const_state["stage_jobs"]:
                b = nb()
                TR(PS[:, b, 0:r0], stg[0:r0, :], ident[0:r0, 0:r0])
                CP("dve", dst[:, 0:r0], PS[:, b, 0:r0])
            for l in range(2):
                ACT(cvec[l][:, 0:4], V[l][:, VLAM:VLAM + 4], AF.Exp, scale=-1.0)
                ACT(cvec[l][:, 0:4], cvec[l][:, 0:4], AF.Ln, bias=1.0, scale=1.0)
                TS("dve", cvec[l][:, 4:8], cvec[l][:, 0:4], -16.0, None, ALU.mult)
                TS("dve", cvec[l][:, 0:4], cvec[l][:, 0:4], -8.0, None, ALU.mult)
                for wi, dst in enumerate((Wr[l], Wi[l])):
                    CP("dve", dst, bd_fs[2 * l + wi])
                b = nb()
                for h in range(4):
                    TR(PS[:, b, h * 128:(h + 1) * 128], wsls[l][:, h, :], ident, inc=(h == 3))
                CP("dve", wst_fs[l], PS[:, b, :].rearrange("p (h t) -> p h t", h=4))
                S.add("pool", lambda e, l=l: e.affine_select(out=WsT[l], in_=wst_fs[l], compare_op=ALU.is_ge, fill=0.0,
                                                             base=0, pattern=[[0, 4], [1, 128]], channel_multiplier=-1),
                      reads=[wst_fs[l]], writes=[WsT[l]])

        def load_kT_from_tok(l, tok):
            for mc in range(2):
                for g in range(2):
                    b = nb()
                    for c4 in range(4):
                        c = g * 4 + c4
                        TR(PS[:, b, c4 * 128:(c4 + 1) * 128], tok[:, mc, c * 128:(c + 1) * 128], ident, inc=(c4 == 3))
                    CP("dve", kT[l][:, g * 4:(g + 1) * 4, mc * 128:(mc + 1) * 128],
                       PS[:, b, :].rearrange("p (c m) -> p c m", c=4))

        def kv_prologue():
            for mc in range(2):
                for g in range(2):
                    b = nb()
                    for c4 in range(4):
                        c = g * 4 + c4
                        TR(PS[:, b, c4 * 128:(c4 + 1) * 128], memtok[:, mc, c * 128:(c + 1) * 128], ident, inc=(c4 == 3))
                    CP("dve", memT[:, g * 4:(g + 1) * 4, mc * 128:(mc + 1) * 128],
                       PS[:, b, :].rearrange("p (c m) -> p c m", c=4))

        def kv_layer(l):
            for i in range(2):
                w = use_piece(0, l, pidx[("wk", i)])
                for jj in range(4):
                    b = nb()
                    for k in range(8):
                        MM(PS[:, b, 0:256], w[:, k, jj * 128:(jj + 1) * 128], memT[:, k, :], k == 0, k == 7)
                    CP("act", kT[l][:, i * 4 + jj, :], PS[:, b, 0:256])
                for mc in range(2):
                    b = nb()
                    for k in range(8):
                        MM(PS[:, b, :], memT[:, k, mc * 128:(mc + 1) * 128], w[:, k, :], k == 0, k == 7)
                    CP("dve", kvh[mc], PS[:, b, :])
                    DMA("sp", dout["p_mk"][l, mc * 128:(mc + 1) * 128, i * 512:(i + 1) * 512], kvh[mc], reads=[kvh[mc]])
            for i in range(2):
                w = use_piece(0, l, pidx[("wv", i)])
                for mc in range(2):
                    b = nb()
                    for k in range(8):
                        MM(PS[:, b, :], memT[:, k, mc * 128:(mc + 1) * 128], w[:, k, :], k == 0, k == 7)
                    CP("dve", kvh[mc], PS[:, b, :])
                    DMA("sp", dout["p_mv"][l, mc * 128:(mc + 1) * 128, i * 512:(i + 1) * 512], kvh[mc], reads=[kvh[mc]])
                    CP("act", vv[l][:, mc, i * 512:(i + 1) * 512], kvh[mc])

        class Ctx:
            pass

        def norm_tail(cx, b, gcol, Vsrc, out):
            N = cx.N
            ACT(lnb[:, :N], PS[:, b, :N], AF.Ln, bias=epsb[:, 0:1], scale=1.0)
            ACT(rstd[:, :N], lnb[:, :N], AF.Exp, scale=-0.5)
            for c in range(8):
                STT(out[:, c, :N], xfm[:, c, :N], Vsrc[:, gcol + c:gcol + c + 1], rstd[:, :N], ALU.mult, ALU.mult)

        def norm(cx, gcol, Vsrc, out):
            N = cx.N
            b = nb()
            for c in range(8):
                MM(PS[:, b, :N], ones_div, sq[:, c, :N], c == 0, c == 7)
            norm_tail(cx, b, gcol, Vsrc, out)

        def mm_block(banks, lhs, rhs, korder, N, M=128):
            for ki, k in enumerate(korder):
                for jj, b in enumerate(banks):
                    MM(PS[:, b, :N], lhs(k, jj), rhs(k), ki == 0, ki == len(korder) - 1)

        def proj_resid(cx, l, kind, rhs_buf, kc, nrm, korder=None):
            N = cx.N
            bs = nb()
            bank_resv.add(bs)
            if kind == "wdn":
                w0 = use_piece(cx.gi, l, cx.pidx[(kind, 0)])
                w1 = use_piece(cx.gi, l, cx.pidx[(kind, 1)], hold=1)
                b01 = [nb(), nb()]
                for k in range(kc):
                    MM(PS[:, b01[0], :N], w0[:, k, :], rhs_buf[:, k, :N], k == 0, k == kc - 1)
                    MM(PS[:, b01[1], :N], w1[:, k, :], rhs_buf[:, k, :N], k == 0, k == kc - 1)
                for j in range(2):
                    TT("dve", xfm[:, j, :N], xfm[:, j, :N], PS[:, b01[j], :N], ALU.add)
                    ACT(sq[:, j, :N], xfm[:, j, :N], AF.Square)
                for j in range(2, 8):
                    w = use_piece(cx.gi, l, cx.pidx[(kind, j)])
                    b = nb()
                    for k in range(kc):
                        MM(PS[:, b, :N], w[:, k, :], rhs_buf[:, k, :N], k == 0, k == kc - 1)
                    TT("dve", xfm[:, j, :N], xfm[:, j, :N], PS[:, b, :N], ALU.add)
                    ACT(sq[:, j, :N], xfm[:, j, :N], AF.Square)
                    if j == 2:
                        MM(PS[:, bs, :N], ones_div, sq[:, 0, :N], True, False)
                    MM(PS[:, bs, :N], ones_div, sq[:, j - 1, :N], False, False)
                MM(PS[:, bs, :N], ones_div, sq[:, 7, :N], False, True)
            else:
                korder = korder or list(range(kc))
                for i in range(2):
                    w = use_piece(cx.gi, l, cx.pidx[(kind, i)])
                    banks = [nb() for _ in range(4)]
                    mm_block(banks, lambda k, jj: w[:, k, jj * 128:(jj + 1) * 128], lambda k: rhs_buf[:, k, :N],
                             korder, N)
                    for jj, b in enumerate(banks):
                        j = i * 4 + jj
                        TT("dve", xfm[:, j, :N], xfm[:, j, :N], PS[:, b, :N], ALU.add)
                        ACT(sq[:, j, :N], xfm[:, j, :N], AF.Square)
                    if i == 1:
                        for j in range(4):
                            MM(PS[:, bs, :N], ones_div, sq[:, j, :N], j == 0, False)
                for j in range(4, 8):
                    MM(PS[:, bs, :N], ones_div, sq[:, j, :N], False, j == 7)
            bank_resv.discard(bs)
            norm_tail(cx, bs, *nrm)

        def mixer(cx, l):
            N, TB, NTB, st = cx.N, cx.TB, cx.NTB, cx.st
            sb = l * 64
            Vl = V[l]
            xa_st = st[:, sb + SCA:sb + SCA + 12].rearrange("p (t j) -> p j t", t=3)
            h_st = st[:, sb + SH:sb + SH + 4]
            t_st = st[:, sb + SCB:sb + SCB + 4].rearrange("p (t c) -> p c t", t=2)
            CP("pool", xa_raw[:, :, 0:3], xa_st)
            CP("pool", t_raw[:, :, 0:2], t_st)

            def zchunk(w, lc):
                b = nb()
                for k in range(8):
                    MM(PS[:, b, :N], w[:, k, lc:lc + 128], xn[:, k, :N], k == 0, k == 7)
                return b

            w = use_piece(cx.gi, l, cx.pidx[("win", 0)])
            xa_banks = [nb() for _ in range(4)]
            mm_block(xa_banks, lambda k, jj: w[:, k, jj * 128:(jj + 1) * 128], lambda k: xn[:, k, :N], range(8), N)
            for j in range(4):
                b = xa_banks[j]
                ACT(xa_raw[:, j, 3:3 + N], PS[:, b, :N], AF.Identity)
                ACT(TA[:, j, :N], PS[:, b, :N], AF.Identity, bias=Vl[:, VCAB + j:VCAB + j + 1],
                    scale=Vl[:, VCAW + 12 + j:VCAW + 13 + j])
            CP("pool", xa_st, xa_raw[:, :, N:N + 3])
            for j in range(4):
                for k in range(3):
                    dst = xa_bf[:, j, :N] if k == 2 else TA[:, j, :N]
                    STT(dst, xa_raw[:, j, k:k + N], Vl[:, VCAW + 4 * k + j:VCAW + 4 * k + j + 1], TA[:, j, :N],
                        ALU.mult, ALU.add)
            w = use_piece(cx.gi, l, cx.pidx[("win", 1)])
            for j in range(4):
                b = zchunk(w, j * 128)
                ACT(yq[:, j, :N], PS[:, b, :N], AF.Gelu_apprx_tanh)
            w = use_piece(cx.gi, l, cx.pidx[("win", 2)])
            for c in range(2):
                b = zchunk(w, c * 128)
                ACT(yq[:, 6 + c, :N], PS[:, b, :N], AF.Gelu_apprx_tanh)
            for tb in range(NTB):
                b = nb()
                for k in range(8):
                    MM(PS[0:TB, b, 0:256], xn[:, k, tb * TB:(tb + 1) * TB], w[:, k, 256:512], k == 0, k == 7)
                ACT(vt[0:TB, tb, :], PS[0:TB, b, 0:256], AF.Gelu_apprx_tanh)
            for tb in range(NTB):
                TT("dve", vsq[0:TB, tb, :], vt[0:TB, tb, :], vt[0:TB, tb, :], ALU.mult)
                S.add("dve", lambda e, tb=tb: e.tensor_reduce(out=ssb[0:TB, tb, :],
                                                              in_=vsq[0:TB, tb, :].rearrange("p (g c) -> p g c", g=4),
                                                              axis=AX.X, op=ALU.add),
                      reads=[vsq[0:TB, tb, :]], writes=[ssb[0:TB, tb, :]])
            for j in range(4):
                b = nb()
                MM(PS[:, b, :N], Wr[l][:, j, :], xa_bf[:, j, :N], True, True)
                ACT(T1[:, j, :N], PS[:, b, :N], AF.Sigmoid, bias=Vl[:, VBR + j:VBR + j + 1], scale=1.0)
            for j in range(4):
                b = nb()
                MM(PS[:, b, :N], Wi[l][:, j, :], xa_bf[:, j, :N], True, True)
                ACT(T2[:, j, :N], PS[:, b, :N], AF.Sigmoid, bias=Vl[:, VBI + j:VBI + j + 1], scale=1.0)
            ssv = ssb[0:TB, 0:NTB, :]
            rsv = rsb[0:TB, 0:NTB, :]
            w = use_piece(cx.gi, l, cx.pidx[("win", 3)])
            for c in range(2):
                b = zchunk(w, c * 128)
                CP("act", xb_sb[:, c, :N], PS[:, b, :N])
            ACT(rsv, ssv, AF.Ln, bias=epsb[0:TB, 0:1], scale=1.0 / 64)
            ACT(rsv, rsv, AF.Exp, scale=-0.5)
            for tb in range(NTB):
                rs_b = rsb[0:TB, tb, :].unsqueeze(2).to_broadcast([TB, 4, 64])
                vt3 = vt[0:TB, tb, :].rearrange("p (g c) -> p g c", g=4)
                if cx.is_sample:
                    TT("dve", vsq[0:TB, tb, :].rearrange("p (g c) -> p g c", g=4), vt3, rs_b, ALU.mult)
                    TT("dve", vn32[0:TB, :], vsq[0:TB, tb, :], gvrep[l][0:TB, :], ALU.mult)
                    CP("dve", vn_bf[0:TB, tb, :], vsq[0:TB, tb, :])
                    DMA("pool", dout["s_vc"][l], vn32[0:TB, :], reads=[vn32[0:TB, :]])
                else:
                    TT("dve", vn_bf[0:TB, tb, :].rearrange("p (g c) -> p g c", g=4), vt3, rs_b, ALU.mult)
            for c in range(2):
                b = zchunk(w, 256 + c * 128)
                CP("act", yq[:, 4 + c, :N], PS[:, b, :N])
            for j in range(4):
                TT("pool", TA[:, j, :N], T2[:, j, :N], xa_bf[:, j, :N], ALU.mult)
            w = use_piece(cx.gi, l, cx.pidx[("win", 4)])
            for c in range(2):
                b = zchunk(w, c * 128)
                TT("dve", t_raw[:, c, 2:2 + N], xb_sb[:, c, :N], PS[:, b, :N], ALU.mult)
            CP("pool", t_st, t_raw[:, :, N:N + 2])
            mb = [nb(), nb()]
            for hp in range(2):
                for tb in range(NTB):
                    for hh in range(2):
                        h = 2 * hp + hh
                        MM(PS[hh * 64:(hh + 1) * 64, mb[hp], tb * TB:(tb + 1) * TB],
                           vn_bf[0:TB, tb, h * 64:(h + 1) * 64], WsT[l][0:TB, h, 0:TB], True, True,
                           tp=(0, hh * 64), wr=PS[:, mb[hp], tb * TB:(tb + 1) * TB])
            for c in range(2):
                TS("dve", zb[:, c, :N], t_raw[:, c, 2:2 + N], Vl[:, VCBW + 4 + c:VCBW + 5 + c], None, ALU.mult)
                for k in range(2):
                    dst = zb_bf[:, c, :N] if k == 1 else zb[:, c, :N]
                    STT(dst, t_raw[:, c, k:k + N], Vl[:, VCBW + 2 * k + c:VCBW + 2 * k + c + 1], zb[:, c, :N],
                        ALU.mult, ALU.add)
            for c in range(2):
                TT("dve", yq[:, 4 + c, :N], yq[:, 4 + c, :N], zb_bf[:, c, :N], ALU.mult)
            for hp in range(2):
                STT(tmpc[:, hp, :N].rearrange("p (b t) -> p b t", b=NTB),
                    PS[:, mb[hp], :N].rearrange("p (b t) -> p b t", b=NTB),
                    Vf[:, 8 + 2 * l + hp:9 + 2 * l + hp],
                    biasbc[l][:, hp, 0:TB].unsqueeze(1).to_broadcast([128, NTB, TB]), ALU.mult, ALU.add)
                TT("dve", yq[:, 6 + hp, :N], yq[:, 6 + hp, :N], tmpc[:, hp, :N], ALU.mult)
            for jp in range(2):
                js = (2 * jp, 2 * jp + 1)
                for j in js:
                    ACT(T3[:, j, :N], T1[:, j, :N], AF.Exp, scale=cvec[l][:, j:j + 1])
                for j in js:
                    ACT(T4[:, j, :N], T1[:, j, :N], AF.Exp, scale=cvec[l][:, 4 + j:5 + j])
                for j in js:
                    ACT(T4[:, j, :N], T4[:, j, :N], AF.Ln, bias=onep[:, 0:1], scale=-1.0)
                for j in js:
                    ACT(T4[:, j, :N], T4[:, j, :N], AF.Exp, scale=0.5)
                for j in js:
                    TT("dve", T2[:, j, :N], T4[:, j, :N], TA[:, j, :N], ALU.mult)
                for j in js:
                    S.add("dve", lambda e, j=j: e.tensor_tensor_scan(out=T1[:, j, :N], data0=T3[:, j, :N],
                                                                     data1=T2[:, j, :N], initial=h_st[:, j:j + 1],
                                                                     op0=ALU.mult, op1=ALU.add),
                          reads=[T3[:, j, :N], T2[:, j, :N], h_st[:, j:j + 1]], writes=[T1[:, j, :N]])
                    TT("dve", yq[:, j, :N], yq[:, j, :N], T1[:, j, :N], ALU.mult)
            CP("pool", h_st, T1[:, :, N - 1])
            proj_resid(cx, l, "wout", yq, 8, (VG_X, V[l], xn), korder=[4, 5, 6, 7, 0, 1, 2, 3])

        def attention(cx, l):
            N = cx.N
            for i in range(2):
                w = use_piece(cx.gi, l, cx.pidx[("wq", i)])
                if i == 0:
                    banks = [nb() for _ in range(4)]
                    mm_block(banks, lambda k, jj: w[:, k, jj * 128:(jj + 1) * 128], lambda k: xn[:, k, :N], range(8), N)
                    for jj, b in enumerate(banks):
                        ACT(yq[:, jj, :N], PS[:, b, :N], AF.Identity)
                    continue
                for jj in range(4):
                    b = nb()
                    for k in range(8):
                        MM(PS[:, b, :N], w[:, k, jj * 128:(jj + 1) * 128], xn[:, k, :N], k == 0, k == 7)
                    ACT(yq[:, i * 4 + jj, :N], PS[:, b, :N], AF.Identity)
            def scores(h):
                for mc in range(2):
                    b = nb()
                    for hc in range(2):
                        MM(PS[:, b, :N], kT[l][:, 2 * h + hc, mc * 128:(mc + 1) * 128], yq[:, 2 * h + hc, :N],
                           hc == 0, hc == 1)
                    ACT(pT[:, h, mc, :N], PS[:, b, :N], AF.Exp, scale=1.0 / 16)

            scores(0)
            for h in range(4):
                if h + 1 < 4:
                    scores(h + 1)
                b = nb()
                for mc in range(2):
                    MM(PS[:, b, :N], ones1, pT[:, h, mc, :N], mc == 0, mc == 1)
                rc = recip[h % 2]
                ACT(rc[:, :N], PS[:, b, :N], AF.Ln)
                ACT(rc[:, :N], rc[:, :N], AF.Exp, scale=-1.0)
                for hc in range(2):
                    b = nb()
                    for mc in range(2):
                        MM(PS[:, b, :N], vv[l][:, mc, h * 256 + hc * 128:h * 256 + (hc + 1) * 128], pT[:, h, mc, :N],
                           mc == 0, mc == 1)
                    TT("dve", ob[:, 2 * h + hc, :N], PS[:, b, :N], rc[:, :N], ALU.mult)
            proj_resid(cx, l, "wo", ob, 8, (VG_FFN, V[l], xn))

        def ffn(cx, l):
            N, st = cx.N, cx.st
            sb = l * 64
            Vl = V[l]
            g_st = st[:, sb + SCF:sb + SCF + 44].rearrange("p (t f) -> p f t", t=2)
            w = None
            for f in range(NFF):
                if f % 2 == 0:
                    w = use_piece(cx.gi, l, cx.pidx[("wup", f // 2)])
                lc = (f % 2) * 128
                r = f % 3
                if f == 0:
                    fb = [nb() for _ in range(4)]
                    mm_block(fb, lambda k, jj: w[:, k, jj % 2, (jj // 2) * 128:(jj // 2 + 1) * 128],
                             lambda k: xn[:, k, :N], range(8), N)
                if f < 2:
                    bg, bu = fb[2 * f], fb[2 * f + 1]
                else:
                    bg = nb()
                    for k in range(8):
                        MM(PS[:, bg, :N], w[:, k, 0, lc:lc + 128], xn[:, k, :N], k == 0, k == 7)
                    bu = nb()
                    for k in range(8):
                        MM(PS[:, bu, :N], w[:, k, 1, lc:lc + 128], xn[:, k, :N], k == 0, k == 7)
                gr = g_raw[r]
                CP("pool", gr[:, 0:2], g_st[:, f, :])
                ACT(gr[:, 2:2 + N], PS[:, bg, :N], AF.Identity)
                ACT(accb[r][:, :N], PS[:, bg, :N], AF.Identity, scale=Vl[:, VCFW + 44 + f:VCFW + 45 + f])
                CP("pool", g_st[:, f, :], gr[:, N:N + 2])
                STT(accb[r][:, :N], gr[:, 1:1 + N], Vl[:, VCFW + 22 + f:VCFW + 23 + f], accb[r][:, :N], ALU.mult, ALU.add)
                STT(accb[r][:, :N], gr[:, 0:N], Vl[:, VCFW + f:VCFW + f + 1], accb[r][:, :N], ALU.mult, ALU.add)
                ACT(slb[r][:, :N], accb[r][:, :N], AF.Silu)
                TT("dve", hbuf[:, f, :N], slb[r][:, :N], PS[:, bu, :N], ALU.mult)
                if f == NFF - 1:
                    ACT(epsb[:, 1:2], epsb[:, 0:1], AF.Ln)
            proj_resid(cx, l, "wdn", hbuf, NFF, (VG_MIX, V[1], xn) if l == 0 else (0, Vf, yfin))

        x_issued = set()

        def xbuf(cx, tb):
            if cx.gi == 0:
                return (xtok + ytok)[tb % 4]
            return xtok[tb % 2]

        def issue_x(cx, tb):
            key = (cx.gi, tb)
            if key in x_issued or tb >= cx.NTB:
                return
            x_issued.add(key)
            TB = cx.TB
            xt = xbuf(cx, tb)
            DMA("pool", xt[0:TB, :], cx.xsrc[cx.t0 + tb * TB:cx.t0 + (tb + 1) * TB, :], writes=[xt[0:TB, :]])

        def load_x(cx, nxt=None):
            N, TB, NTB = cx.N, cx.TB, cx.NTB
            issue_x(cx, 0)
            issue_x(cx, 1)
            for tb in range(NTB):
                xt = xbuf(cx, tb)
                for g in range(2):
                    b = nb()
                    for c4 in range(4):
                        c = g * 4 + c4
                        TR(PS[:, b, c4 * TB:(c4 + 1) * TB], xt[0:TB, c * 128:(c + 1) * 128], ident[0:TB, 0:TB],
                           inc=(c4 == 3))
                    src = PS[:, b, 0:4 * TB].rearrange("p (c t) -> p c t", c=4)
                    CP("dve", xfm[:, g * 4:(g + 1) * 4, tb * TB:(tb + 1) * TB], src)
                    ACT(sq[:, g * 4:(g + 1) * 4, tb * TB:(tb + 1) * TB], src, AF.Square)
                if tb + 2 < NTB:
                    issue_x(cx, tb + 2)
                elif nxt is not None:
                    issue_x(nxt, tb + 2 - NTB)

        def store_y(cx):
            N, TB, NTB = cx.N, cx.TB, cx.NTB
            for tb in range(NTB):
                yt = ytok[tb % 2]
                for g in range(2):
                    b = nb()
                    for c4 in range(4):
                        c = g * 4 + c4
                        TR(PS[0:TB, b, c4 * 128:(c4 + 1) * 128], yfin[:, c, tb * TB:(tb + 1) * TB], ident, inc=(c4 == 3))
                    CP("act" if g == 0 else "dve", yt[0:TB, g * 512:(g + 1) * 512], PS[0:TB, b, :])
                DMA("pool", cx.ydst[cx.t0 + tb * TB:cx.t0 + (tb + 1) * TB, :], yt[0:TB, :], reads=[yt[0:TB, :]])

        def store_states(st, names):
            b = nb()
            TR(PS[:, b, 0:128], st, ident)
            CP("dve", stage, PS[:, b, 0:128])
            for l in range(2):
                r0 = l * 64
                DMA("pool", dout[names[0]][l].rearrange("t (c p) -> (t c) p", p=128), stage[r0 + SCA:r0 + SCA + 12, :],
                    reads=[stage])
                DMA("pool", dout[names[1]][l].rearrange("(c p) -> c p", p=128), stage[r0 + SH:r0 + SH + 4, :],
                    reads=[stage])
                DMA("pool", dout[names[2]][l].rearrange("t (c p) -> (t c) p", p=128), stage[r0 + SCB:r0 + SCB + 4, :],
                    reads=[stage])
                DMA("pool", dout[names[3]][l].rearrange("t (c p) -> (t c) p", p=128), stage[r0 + SCF:r0 + SCF + 44, :],
                    reads=[stage])

        pidx = {(kind, i): n for n, (kind, i, kc, ncol) in enumerate(PIECES_X)}
        ctxs = []
        for gi, (gk, ti) in enumerate(groups):
            cx = Ctx()
            cx.gi = gi
            cx.gk, cx.ti = gk, ti
            cx.pidx = pidx
            cx.is_sample = gk == "s"
            if gk == "p":
                cx.N, cx.TB, cx.NTB = NT, 128, NT // 128
                cx.xsrc, cx.ydst, cx.t0, cx.st = din["xp"], dout["yp"], ti * NT, ST[0]
            else:
                cx.N, cx.TB, cx.NTB = DEC, DEC, 1
                cx.xsrc, cx.ydst, cx.t0, cx.st = din["xs"], dout["ys"], 0, ST[1]
            ctxs.append(cx)
        import os
        if os.environ.get("KV_FIRST"):
            kv_prologue()
        setup_consts_dma()
        for tb in range(4):
            issue_x(ctxs[0], tb)
        wstate["defer"] = []
        while wstate["loaded"] < min(len(seq), NSLOT):
            record_load(wstate["loaded"])
            wstate["loaded"] += 1
        setup_consts_compute()
        for wb in wstate["defer"]:
            wb()
        wstate["defer"] = None
        if not os.environ.get("KV_FIRST"):
            kv_prologue()
        for gi, cx in enumerate(ctxs):
            gk, ti = cx.gk, cx.ti
            nxt = ctxs[gi + 1] if gi + 1 < len(ctxs) else None
            if gk == "s":
                for l in range(2):
                    DMA("sp", memtok, din["ck"][l].rearrange("(m p) f -> p m f", p=128), writes=[memtok])
                    load_kT_from_tok(l, memtok)
                    DMA("pool", vv[l], din["cv"][l].rearrange("(m p) f -> p m f", p=128), writes=[vv[l]])
            if gi == 0:
                load_x(cx, nxt)
                norm(cx, VG_MIX, V[0], xn)
            for l in range(2):
                mixer(cx, l)
                if gi == 0:
                    kv_layer(l)
                attention(cx, l)
                ffn(cx, l)
            if nxt is not None:
                load_x(nxt, ctxs[gi + 2] if gi + 2 < len(ctxs) else None)
                norm(nxt, VG_MIX, V[0], xn)
            store_y(cx)
            if gk == "p" and ti == n_ptiles - 1:
                store_states(ST[0], ["p_ca", "p_h", "p_cb", "p_cf"])
            if gk == "s":
                store_states(ST[1], ["s_ca", "s_h", "s_cb", "s_cf"])

        with nc.Block() as block:
            S.emit(block)
    return nc


_OUT_ORDER = ["yp", "ys", "p_ca", "p_h", "p_cb", "p_cf", "p_mk", "p_mv", "s_ca", "s_h", "s_cb", "s_cf", "s_vc"]


def kernel(x_prompt, x_sample, mem_prompt, cache_mem_k, cache_mem_v, state_conv_a, state_h_a, state_conv_b,
           state_conv_ffn, **weights):
    n = 8
    f = lambda a: np.ascontiguousarray(np.asarray(a, dtype=np.float32))
    wmap = {k: f(weights[k]) for k in W_NAMES}
    in_maps = []
    for b in range(n):
        m = dict(wmap)
        m["xp"] = f(x_prompt[b])
        m["xs"] = f(x_sample[b])
        m["mem"] = f(mem_prompt[b])
        m["ck"] = f(np.asarray(cache_mem_k)[:, b].reshape(2, NMEM, D))
        m["cv"] = f(np.asarray(cache_mem_v)[:, b].reshape(2, NMEM, D))
        m["sca"] = f(np.asarray(state_conv_a)[:, b])
        m["sh"] = f(np.asarray(state_h_a)[:, b])
        m["scb"] = f(np.asarray(state_conv_b)[:, b])
        m["scf"] = f(np.asarray(state_conv_ffn)[:, b])
        in_maps.append(m)
    nc = build_program()
    res = run_bass_kernel_spmd(nc, in_maps, core_ids=list(range(n)))
    r = res.results
    outs = {}
    outs["yp"] = np.stack([r[b]["yp"] for b in range(n)], 0)
    outs["ys"] = np.stack([r[b]["ys"] for b in range(n)], 0)
    for k in ["p_ca", "p_h", "p_cb", "p_cf", "s_ca", "s_h", "s_cb", "s_cf", "s_vc"]:
        outs[k] = np.stack([r[b][k] for b in range(n)], 1)
    for k in ["p_mk", "p_mv"]:
        outs[k] = np.stack([r[b][k] for b in range(n)], 1).reshape(2, n, NMEM, 4, 256)
    return tuple(np.ascontiguousarray(outs[k], dtype=np.float32) for k in _OUT_ORDER)
```
